# Optimizing a Trainium2 kernel written in Bass

```python
import math
import jax, jax.numpy as jnp
from jax import lax
import numpy as np

D_MODEL = 1024
BATCH = 8
SEQ = 8192
DEPTH = 1

D_MIX = D_MODEL
D_ATTN = D_MIX // 2
D_SSM = D_MIX - D_ATTN
N_HEADS = 4
QK_NOPE = 128
QK_ROPE = 64
V_HEAD = D_ATTN // N_HEADS
Q_LORA = 384
KV_LORA = 256
ROPE_THETA = 10000.0
Q_BLOCK = 128
MAX_POS_OFFSET = 4096
SSM_GROUP = 16
N_SSM_GROUPS = D_SSM // SSM_GROUP
SSM_STATE = 64
DT_MIN = 1e-3
DT_MAX = 1e-1
SSM_C_STD = 0.5
D_FF = 2816
CONV_W = 3
EPS = 1e-6
IN_COLS = Q_LORA + KV_LORA + QK_ROPE + D_SSM
OFF_KV = Q_LORA
OFF_KR = Q_LORA + KV_LORA
OFF_U = Q_LORA + KV_LORA + QK_ROPE

kernel_name = 'hymba_mla_s5_convglu_adaln'


def _rmsnorm(x, g):
    xf = x.astype(jnp.float32)
    xf = xf * lax.rsqrt(jnp.mean(xf * xf, axis=-1, keepdims=True) + EPS)
    return (xf * g.astype(jnp.float32)).astype(x.dtype)


def _modulate(h, shift, scale):
    return h * (1.0 + scale[:, None, :]) + shift[:, None, :]


def _rope_tables(positions):
    inv_freq = ROPE_THETA ** (-jnp.arange(0, QK_ROPE, 2, dtype=jnp.float32) / QK_ROPE)
    ang = positions.astype(jnp.float32)[..., None] * inv_freq
    return jnp.cos(ang), jnp.sin(ang)


def _apply_rope(x, cos, sin):
    xf = x.astype(jnp.float32)
    x1, x2 = xf[..., :QK_ROPE // 2], xf[..., QK_ROPE // 2:]
    return jnp.concatenate([x1 * cos - x2 * sin, x1 * sin + x2 * cos], axis=-1).astype(x.dtype)


def _mla(zq, zkv, zkr, positions, q_norm_g, w_uq, kv_norm_g, w_ukv):
    bsz, seq, _ = zq.shape
    cos, sin = _rope_tables(positions)
    q = (_rmsnorm(zq, q_norm_g) @ w_uq).reshape(bsz, seq, N_HEADS, QK_NOPE + QK_ROPE)
    q_nope = q[..., :QK_NOPE]
    q_rope = _apply_rope(q[..., QK_NOPE:], cos[:, :, None, :], sin[:, :, None, :])
    kv = (_rmsnorm(zkv, kv_norm_g) @ w_ukv).reshape(bsz, seq, N_HEADS, QK_NOPE + V_HEAD)
    k_nope, v = kv[..., :QK_NOPE], kv[..., QK_NOPE:]
    k_rope = _apply_rope(zkr, cos, sin)
    n_blk = seq // Q_BLOCK
    qn_blk = q_nope.reshape(bsz, n_blk, Q_BLOCK, N_HEADS, QK_NOPE).transpose(1, 0, 2, 3, 4)
    qr_blk = q_rope.reshape(bsz, n_blk, Q_BLOCK, N_HEADS, QK_ROPE).transpose(1, 0, 2, 3, 4)
    key_idx = jnp.arange(seq)
    scale = (QK_NOPE + QK_ROPE) ** -0.5

    def one_block(args):
        qn, qr, blk = args
        s = (jnp.einsum('bqhd,bkhd->bhqk', qn, k_nope)
             + jnp.einsum('bqhr,bkr->bhqk', qr, k_rope)).astype(jnp.float32) * scale
        q_idx = blk * Q_BLOCK + jnp.arange(Q_BLOCK)
        s = jnp.where(key_idx[None, :] <= q_idx[:, None], s, -1e30)
        p = jax.nn.softmax(s, axis=-1).astype(v.dtype)
        return jnp.einsum('bhqk,bkhd->bqhd', p, v)

    o = lax.map(one_block, (qn_blk, qr_blk, jnp.arange(n_blk)))
    return o.transpose(1, 0, 2, 3, 4).reshape(bsz, seq, N_HEADS * V_HEAD)


def _ssm_combine(e1, e2):
    a1r, a1i, b1r, b1i = e1
    a2r, a2i, b2r, b2i = e2
    return (a2r * a1r - a2i * a1i,
            a2r * a1i + a2i * a1r,
            a2r * b1r - a2i * b1i + b2r,
            a2r * b1i + a2i * b1r + b2i)


def _s5(zu, lam_re, lam_im, log_dt, b_re, b_im, c_re, c_im, d_skip, w_glu, b_glu):
    bsz, seq, _ = zu.shape
    f32 = jnp.float32
    uf = zu.astype(f32).reshape(bsz, seq, N_SSM_GROUPS, SSM_GROUP)
    lr = jnp.minimum(lam_re.astype(f32), -1e-4)
    li = lam_im.astype(f32)
    dt = jnp.exp(log_dt.astype(f32))[:, None]
    mag = jnp.exp(lr * dt)
    ab_re = mag * jnp.cos(li * dt)
    ab_im = mag * jnp.sin(li * dt)
    den = lr * lr + li * li
    nr, ni = ab_re - 1.0, ab_im
    z_re = ((nr * lr + ni * li) / den)[..., None]
    z_im = ((ni * lr - nr * li) / den)[..., None]
    br, bi = b_re.astype(f32), b_im.astype(f32)
    bb_re = z_re * br - z_im * bi
    bb_im = z_re * bi + z_im * br
    bu_re = jnp.einsum('bsgh,gph->sbgp', uf, bb_re)
    bu_im = jnp.einsum('bsgh,gph->sbgp', uf, bb_im)
    a_re = jnp.broadcast_to(ab_re[None, None], (seq, 1, N_SSM_GROUPS, SSM_STATE))
    a_im = jnp.broadcast_to(ab_im[None, None], (seq, 1, N_SSM_GROUPS, SSM_STATE))
    _, _, xr, xi = lax.associative_scan(_ssm_combine, (a_re, a_im, bu_re, bu_im), axis=0)
    y = (jnp.einsum('sbgp,ghp->bsgh', xr, c_re.astype(f32))
         - jnp.einsum('sbgp,ghp->bsgh', xi, c_im.astype(f32))
         + d_skip.astype(f32) * uf)
    y = jax.nn.gelu(y.reshape(bsz, seq, D_SSM))
    gl = y @ w_glu.astype(f32) + b_glu.astype(f32)
    out = gl[..., :D_SSM] * jax.nn.sigmoid(gl[..., D_SSM:])
    return out.astype(zu.dtype)


def _causal_dwconv(x, w, b):
    seq = x.shape[1]
    xp = jnp.pad(x, ((0, 0), (CONV_W - 1, 0), (0, 0)))
    y = b
    for k in range(CONV_W):
        y = y + w[k] * xp[:, k:k + seq, :]
    return y


def setup_inputs(seed: int = 0) -> dict:
    key = jax.random.key(seed)
    ks = jax.random.split(key, 32)
    f32 = jnp.float32
    L, G, P, H = DEPTH, N_SSM_GROUPS, SSM_STATE, SSM_GROUP

    def nrm(k, shape, std):
        return jax.random.normal(k, shape, f32) * std

    x = nrm(ks[0], (BATCH, SEQ, D_MODEL), 1.0)
    c = nrm(ks[1], (BATCH, D_MODEL), 1.0)
    offsets = jax.random.randint(ks[2], (BATCH, 1), 0, MAX_POS_OFFSET, dtype=jnp.int32)
    positions = offsets + jnp.arange(SEQ, dtype=jnp.int32)[None, :]
    return {
        'x': x,
        'c': c,
        'positions': positions,
        'w_mod': nrm(ks[3], (L, D_MODEL, 6 * D_MODEL), D_MODEL ** -0.5),
        'b_mod': nrm(ks[4], (L, 6 * D_MODEL), 0.02),
        'ln1_g': 1.0 + nrm(ks[5], (L, D_MODEL), 0.02),
        'w_in': nrm(ks[6], (L, D_MODEL, IN_COLS), D_MODEL ** -0.5),
        'q_norm_g': 1.0 + nrm(ks[7], (L, Q_LORA), 0.02),
        'w_uq': nrm(ks[8], (L, Q_LORA, N_HEADS * (QK_NOPE + QK_ROPE)), Q_LORA ** -0.5),
        'kv_norm_g': 1.0 + nrm(ks[9], (L, KV_LORA), 0.02),
        'w_ukv': nrm(ks[10], (L, KV_LORA, N_HEADS * (QK_NOPE + V_HEAD)), KV_LORA ** -0.5),
        'ssm_lam_re': -0.5 + nrm(ks[11], (L, G, P), 0.01),
        'ssm_lam_im': jnp.tile(math.pi * jnp.arange(P, dtype=f32), (L, G, 1)),
        'ssm_log_dt': jax.random.uniform(ks[12], (L, G), f32, math.log(DT_MIN), math.log(DT_MAX)),
        'ssm_b_re': nrm(ks[13], (L, G, P, H), (2 * H) ** -0.5),
        'ssm_b_im': nrm(ks[14], (L, G, P, H), (2 * H) ** -0.5),
        'ssm_c_re': nrm(ks[15], (L, G, H, P), SSM_C_STD),
        'ssm_c_im': nrm(ks[16], (L, G, H, P), SSM_C_STD),
        'ssm_d': nrm(ks[17], (L, G, H), 1.0),
        'w_glu': nrm(ks[18], (L, D_SSM, 2 * D_SSM), D_SSM ** -0.5),
        'b_glu': nrm(ks[19], (L, 2 * D_SSM), 0.02),
        'attn_out_g': 1.0 + nrm(ks[20], (L, D_ATTN), 0.02),
        'ssm_out_g': 1.0 + nrm(ks[21], (L, D_SSM), 0.02),
        'w_out': nrm(ks[22], (L, D_MIX, D_MODEL), D_MIX ** -0.5),
        'ln2_g': 1.0 + nrm(ks[23], (L, D_MODEL), 0.02),
        'w_up': nrm(ks[24], (L, D_MODEL, 2 * D_FF), D_MODEL ** -0.5),
        'conv_w': nrm(ks[25], (L, CONV_W, D_FF), CONV_W ** -0.5),
        'conv_b': nrm(ks[26], (L, D_FF), 0.02),
        'w_down': nrm(ks[27], (L, D_FF, D_MODEL), D_FF ** -0.5),
        'final_g': 1.0 + nrm(ks[28], (D_MODEL,), 0.02),
    }


def reference(x, c, positions, w_mod, b_mod, ln1_g, w_in, q_norm_g, w_uq, kv_norm_g, w_ukv,
              ssm_lam_re, ssm_lam_im, ssm_log_dt, ssm_b_re, ssm_b_im, ssm_c_re, ssm_c_im,
              ssm_d, w_glu, b_glu, attn_out_g, ssm_out_g, w_out, ln2_g, w_up, conv_w, conv_b,
              w_down, final_g):
    cond = jax.nn.silu(c)
    for l in range(DEPTH):
        mod = cond @ w_mod[l] + b_mod[l]
        sh_a, sc_a, g_a, sh_f, sc_f, g_f = jnp.split(mod, 6, axis=-1)
        h = _modulate(_rmsnorm(x, ln1_g[l]), sh_a, sc_a)
        z = h @ w_in[l]
        a = _mla(z[..., :OFF_KV], z[..., OFF_KV:OFF_KR], z[..., OFF_KR:OFF_U], positions,
                 q_norm_g[l], w_uq[l], kv_norm_g[l], w_ukv[l])
        s = _s5(z[..., OFF_U:], ssm_lam_re[l], ssm_lam_im[l], ssm_log_dt[l], ssm_b_re[l],
                ssm_b_im[l], ssm_c_re[l], ssm_c_im[l], ssm_d[l], w_glu[l], b_glu[l])
        m = jnp.concatenate([_rmsnorm(a, attn_out_g[l]), _rmsnorm(s, ssm_out_g[l])], axis=-1) @ w_out[l]
        x = x + g_a[:, None, :] * m
        h = _modulate(_rmsnorm(x, ln2_g[l]), sh_f, sc_f)
        up = h @ w_up[l]
        gate_in, val = up[..., :D_FF], up[..., D_FF:]
        f = (jax.nn.gelu(_causal_dwconv(gate_in, conv_w[l], conv_b[l])) * val) @ w_down[l]
        x = x + g_f[:, None, :] * f
    return _rmsnorm(x, final_g)
```

```python
import math
from contextlib import ExitStack

import numpy as np
import concourse.bass as bass
import concourse.mybir as mybir
from concourse.bass_utils import run_bass_kernel_spmd

F32 = mybir.dt.float32
BF16 = mybir.dt.bfloat16
I32 = mybir.dt.int32
ALU = mybir.AluOpType
AF = mybir.ActivationFunctionType
AX = mybir.AxisListType

D = 1024
NH = 4
QL = 384
KVL = 256
DSSM = 512
DFF = 2816
NKF = DFF // 128
EPS = 1e-6
T = 16
SCALE = (128 + 64) ** -0.5
MAGIC = 12582912.0
C1 = 6.28125
C2 = 0.0019353071693331003
C3 = 1.0253131677018246e-11
SINC = [-0.16666666666666666, 0.008333333333333333, -0.0001984126984126984,
        2.7557319223985893e-06, -2.505210838544172e-08, 1.6059043836821613e-10]
COSC = [-0.5, 0.041666666666666664, -0.001388888888888889, 2.48015873015873e-05,
        -2.755731922398589e-07, 2.08767569878681e-09]

ENGS = ("pe", "act", "dve", "pool", "sp")
SKIP = set()
STOP = 0


class _Stop(Exception):
    pass


def stage(n):
    if STOP and n > STOP:
        raise _Stop()

PP = {}
_o = 0
for _n, _w in (("bmod", 48), ("ln1", 8), ("ln2", 8), ("qg", 3), ("kvg", 2), ("ag", 4), ("sg", 4),
               ("bglu", 8), ("convw", 66), ("convb", 22), ("invf", 1), ("sgn", 1), ("ssmd", 4),
               ("halfpi", 1), ("eps", 1)):
    PP[_n] = (_o, _o + _w)
    _o += _w
NPP = _o


class Buf:
    __slots__ = ("name", "w", "r")

    def __init__(self, name=""):
        self.name = name
        self.w = []
        self.r = []


class KB:
    def __init__(self, nc, stack):
        self.nc = nc
        self.stack = stack
        self.streams = {e: [] for e in ENGS}
        self.sems = {}
        self.cnt = {}
        self.seen = {e: {} for e in ENGS}
        for e in ENGS:
            self._mksem("E_" + e)
        self.ndma = 0
        self.nswdma = 0
        self.pending = {}
        self.NDSEM = 24

    def _mksem(self, key):
        if key not in self.sems:
            self.sems[key] = self.stack.enter_context(self.nc.semaphore(key))
            self.cnt[key] = 0
        return self.sems[key]

    def _waits(self, eng, reads, writes):
        need = {}
        for b in reads:
            for (k, v) in b.w:
                if need.get(k, 0) < v:
                    need[k] = v
        for b in writes:
            for (k, v) in b.w:
                if need.get(k, 0) < v:
                    need[k] = v
            for (k, v) in b.r:
                if need.get(k, 0) < v:
                    need[k] = v
        out = []
        seen = self.seen[eng]
        for k, v in need.items():
            if seen.get(k, 0) >= v:
                continue
            seen[k] = v
            out.append((k, v))
        return out

    def _commit(self, tok, reads, writes):
        for b in reads:
            b.r.append(tok)
            if len(b.r) > 16:
                d = {}
                for (k, v) in b.r:
                    if d.get(k, 0) < v:
                        d[k] = v
                b.r = list(d.items())
        for b in writes:
            b.w = [tok]
            b.r = []

    def op(self, eng, fn, reads=(), writes=(), signal=True):
        waits = self._waits(eng, reads, writes)
        st = self.streams[eng]
        for (k, v) in waits:
            st.append(("w", self.sems[k], v))
        if signal:
            key = "E_" + eng
            self.cnt[key] += 1
            tok = (key, self.cnt[key])
            st.append(("i", fn, self.sems[key], 1))
            pend = self.pending.get(eng)
            if pend:
                for (r_, w_) in pend:
                    self._commit(tok, r_, w_)
                self.pending[eng] = []
            self._commit(tok, reads, writes)
            return tok
        st.append(("i", fn, None, 0))
        self.pending.setdefault(eng, []).append((list(reads), list(writes)))
        return None

    def dma(self, q, out_ap, in_ap, reads=(), writes=(), **kw):
        waits = self._waits(q, reads, writes)
        st = self.streams[q]
        for (k, v) in waits:
            st.append(("w", self.sems[k], v))
        if q == "pool":
            semkey = "S_%d" % self.nswdma
            self.nswdma += 1
        else:
            semkey = "D_%d" % (self.ndma % self.NDSEM)
            self.ndma += 1
        self._mksem(semkey)
        if self.cnt[semkey] > 0 and self.seen[q].get(semkey, 0) < self.cnt[semkey]:
            self.seen[q][semkey] = self.cnt[semkey]
            st.append(("w", self.sems[semkey], self.cnt[semkey]))
        self.cnt[semkey] += 16
        tok = (semkey, self.cnt[semkey])
        st.append(("i", (lambda e, o=out_ap, i=in_ap, kw=kw: e.dma_start(out=o, in_=i, **kw)),
                   self.sems[semkey], 16))
        self._commit(tok, reads, writes)
        return tok

    def barrier(self, engs=ENGS):
        for e in engs:
            st = self.streams[e]
            for k, v in self.cnt.items():
                if v > 0 and self.seen[e].get(k, 0) < v:
                    self.seen[e][k] = v
                    st.append(("w", self.sems[k], v))

    def emit(self):
        names = {"pe": "tensor", "act": "scalar", "dve": "vector", "pool": "gpsimd", "sp": "sync"}
        with self.nc.Block() as block:
            for e in ENGS:
                stream = self.streams[e]

                def body(engobj, stream=stream):
                    for it in stream:
                        if it[0] == "w":
                            engobj.wait_ge(it[1], it[2])
                        else:
                            ins = it[1](engobj)
                            if it[2] is not None:
                                ins.then_inc(it[2], it[3])
                getattr(block, names[e])(body)

    def act(self, out, in_, func, reads, writes, **kw):
        return self.op("act", lambda e: e.activation(out=out, in_=in_, func=func, **kw), reads, writes)

    def tt(self, eng, out, in0, in1, op, reads, writes):
        return self.op(eng, lambda e: e.tensor_tensor(out=out, in0=in0, in1=in1, op=op), reads, writes)

    def ts(self, eng, out, in0, s1, s2, op0, op1, reads, writes):
        if op1 is None:
            return self.op(eng, lambda e: e.tensor_scalar(out=out, in0=in0, scalar1=s1, scalar2=None, op0=op0),
                           reads, writes)
        return self.op(eng, lambda e: e.tensor_scalar(out=out, in0=in0, scalar1=s1, scalar2=s2, op0=op0, op1=op1),
                       reads, writes)

    def stt(self, eng, out, in0, scalar, in1, op0, op1, reads, writes):
        return self.op(eng, lambda e: e.scalar_tensor_tensor(out=out, in0=in0, scalar=scalar, in1=in1,
                                                             op0=op0, op1=op1), reads, writes)

    def cp(self, eng, out, in_, reads, writes):
        if eng == "act":
            return self.op("act", lambda e: e.copy(out=out, in_=in_), reads, writes)
        return self.op(eng, lambda e: e.tensor_copy(out=out, in_=in_), reads, writes)

    def mm(self, out, lhsT, rhs, start, stop, reads=(), writes=(), signal=False, tp=None):
        if tp is None:
            fn = lambda e: e.matmul(out, lhsT=lhsT, rhs=rhs, start=start, stop=stop)
        else:
            fn = lambda e: e.matmul(out, lhsT=lhsT, rhs=rhs, start=start, stop=stop, tile_position=tp)
        return self.op("pe", fn, reads, writes, signal=signal)

    def tr(self, out, in_, ident, reads=(), writes=(), signal=False):
        return self.op("pe", lambda e: e.transpose(out=out, in_=in_, identity=ident), reads, writes, signal=signal)


class Ring:
    def __init__(self, tiles):
        self.tiles = tiles
        self.bufs = [Buf() for _ in tiles]
        self.i = 0

    def next(self):
        t, b = self.tiles[self.i], self.bufs[self.i]
        self.i = (self.i + 1) % len(self.tiles)
        return t, b


def build(S, debug=False, phases=(0, 1, 2, 3, 4)):
    assert S % 512 == 0
    NT = S // 512
    NC = S // T
    nc = bass.Bass("TRN2", target_bir_lowering=False)

    def din(name, shape, dt=F32):
        return nc.dram_tensor(name, list(shape), dt, kind="ExternalInput").ap()

    def dscr(name, shape, dt):
        return nc.dram_tensor(name, list(shape), dt, kind="Internal").ap()

    x = din("x", [S, D])
    pos = din("pos", [1, S], I32)
    cpm = din("cpm", [128, 8])
    ppd = din("pp", [128, NPP])
    bmod_row = din("bmod_row", [1, 6 * D])
    fing = din("fing", [1, D])
    w_mod = din("w_mod", [D, 6 * D])
    w_in = din("w_in", [D, 1280])
    w_uq = din("w_uq", [QL, 1024])
    w_ukn = din("w_ukn", [KVL, 512])
    w_uv = din("w_uv", [KVL, 512])
    w_glu = din("w_glu", [DSSM, 1024])
    w_out = din("w_out", [D, D])
    w_up = din("w_up", [D, 2 * DFF])
    w_down = din("w_down", [DFF, D])
    lam2 = din("lam2", [128, 3, 16])
    c2 = din("c2", [128, 2, 16, 32])
    b2 = din("b2", [128, 2, 16, 128])
    lam1 = din("lam1", [128, 3, 4, 64])
    b1 = din("b1", [128, 2, 4, 128])
    cst = din("cst", [128, 4, 128])
    out = nc.dram_tensor("out", [S, D], F32, kind="ExternalOutput").ap()

    dQN = dscr("dQN", [128, 4, S], BF16)
    dQR = dscr("dQR", [128, 4, S], BF16)
    dKN = dscr("dKN", [128, 4, S], BF16)
    dKR = dscr("dKR", [64, S], BF16)
    dV = dscr("dV", [S // 128, 128, 512], BF16)
    dAN = dscr("dAN", [128, 4, S], BF16)
    dSN = dscr("dSN", [128, 4, S], BF16)
    dWUP = dscr("dWUP", [D, 2 * DFF], BF16)
    dWIN = dscr("dWIN", [D, 1280], BF16)
    dWUQ = dscr("dWUQ", [QL, 1024], BF16)
    dWUKN = dscr("dWUKN", [KVL, 512], BF16)
    dWUV = dscr("dWUV", [KVL, 512], BF16)
    dWGLU = dscr("dWGLU", [DSSM, 1024], BF16)
    dWOUT = dscr("dWOUT", [D, D], BF16)
    dWDN = dscr("dWDN", [DFF, D], BF16)
    bWc = Buf()
    bQ = [Buf() for _ in range(NT)]
    bK = [Buf() for _ in range(NT)]
    bAN = [Buf() for _ in range(NT)]
    bSN = [Buf() for _ in range(NT)]
    bWUP = Buf()

    dbg = {}

    def dout(name, shape, dt=F32):
        t = nc.dram_tensor(name, list(shape), dt, kind="ExternalOutput").ap()
        dbg[name] = t
        return t

    with ExitStack() as top:
        kb = KB(nc, top)

        def sb(st, name, shape, dt):
            return st.enter_context(nc.sbuf_tensor(name, list(shape), dt))

        def pm(st, name, shape, dt=F32):
            return st.enter_context(nc.psum_tensor(name, list(shape), dt))

        def ring(st, name, shape, dt, n, psum=False):
            return Ring([(pm if psum else sb)(st, "%s%d" % (name, i), shape, dt) for i in range(n)])

        ppt = sb(top, "ppt", [128, NPP], F32); b_pp = Buf()
        ident_b = sb(top, "ident_b", [128, 128], BF16)
        ident_f = sb(top, "ident_f", [128, 128], F32)
        tri_b = sb(top, "tri_b", [128, 128], BF16)
        selA = sb(top, "selA", [128, 128], BF16)
        selB = sb(top, "selB", [128, 128], BF16)
        ones_b = sb(top, "ones_b", [128, 128], BF16)
        b_c = Buf()
        modf = sb(top, "modf", [128, 48], F32); b_modf = Buf()
        a1 = sb(top, "a1", [128, 8], F32)
        a2 = sb(top, "a2", [128, 8], F32)
        qgs = sb(top, "qgs", [128, 3], F32)
        gab = sb(top, "gab", [128, D], F32)
        gfb = sb(top, "gfb", [128, D], F32)
        b_gb = Buf()
        negm = sb(top, "negm", [128, 4], F32); b_negm = Buf()

        def P_(name, i=None):
            a, b = PP[name]
            if i is None:
                return ppt[:, a:b]
            return ppt[:, a + i:a + i + 1]

        kb.dma("sp", ppt[:], ppd, writes=[b_pp])
        kb.dma("pool", ident_b[:], cst[:, 0, :], writes=[b_c])
        kb.dma("sp", ident_f[:], cst[:, 0, :], writes=[b_c])
        kb.dma("pool", tri_b[:], cst[:, 1, :], writes=[b_c])
        kb.dma("pool", selA[:], cst[:, 2, :], writes=[b_c])
        kb.dma("pool", selB[:], cst[:, 3, :], writes=[b_c])
        kb.op("dve", lambda e: e.memset(ones_b[:], 1.0), writes=[b_c])

        klag = dscr("dKLAG", [128, 4, T, 128], BF16); b_klag = Buf()
        bp = dscr("dBP", [128, 4, T, 2, 128], BF16); b_bp = Buf()
        cpt = dscr("dCPT", [128, 16, T + 1, 2, 32], BF16); b_cpt = Buf()
        NLD = max(1, int(math.log2(NC)))
        ld = sb(top, "ld", [128, 16, NLD + 1, 3], F32); b_ld = Buf()
        dU = dscr("dU", [128, 4, S], BF16)
        bU = [Buf() for _ in range(NT)]

        for dst_, src_ in ((dWIN, w_in), (dWUQ, w_uq), (dWUKN, w_ukn), (dWUV, w_uv), (dWGLU, w_glu)):
            kb.dma("pool", dst_, src_, writes=[bWc])
        if 4 in phases:
            kb.dma("pool", dWOUT, w_out, writes=[bWc])
            for c in range(2):
                kb.dma("pool", dWDN[c * 1408:(c + 1) * 1408, :], w_down[c * 1408:(c + 1) * 1408, :], writes=[bWc])
            for c in range(4):
                kb.dma("pool", dWUP[c * 256:(c + 1) * 256, :], w_up[c * 256:(c + 1) * 256, :], writes=[bWUP])
        with ExitStack() as st:
            def mod_part():
                cpt_ = sb(st, "cpm_t", [128, 8], F32); b_cond = Buf()
                cond = sb(st, "cond", [128, 8], F32)
                kb.dma("sp", cpt_[:], cpm, writes=[b_cond])
                kb.act(cond[:], cpt_[:], AF.Silu, [b_cond], [b_cond])
                wm = ring(st, "wm", [128, 8, 512], F32, 2)
                psm = pm(st, "psm", [128, 512]); b_psm = Buf()
                psb = pm(st, "psb", [128, 2, 512]); b_psb = Buf()
                for v in range(12):
                    wt, wb = wm.next()
                    kb.dma("sp", wt[:], w_mod[:, v * 512:(v + 1) * 512].rearrange("(kt p) n -> p kt n", p=128), writes=[wb])
                    for m in range(4):
                        for kt in range(8):
                            kb.mm(psm[:, v * 4 + m:v * 4 + m + 1], wt[:, kt, m * 128:(m + 1) * 128], cond[:, kt:kt + 1],
                                  kt == 0, kt == 7, reads=[wb, b_cond], writes=[b_psm], signal=(kt == 7))
                kb.tt("dve", modf[:], psm[:, 0:48], P_("bmod"), ALU.add, [b_psm, b_pp], [b_modf])
                kb.stt("dve", a1[:], modf[:, 8:16], 1.0, P_("ln1"), ALU.add, ALU.mult, [b_modf, b_pp], [b_modf])
                kb.stt("dve", a2[:], modf[:, 32:40], 1.0, P_("ln2"), ALU.add, ALU.mult, [b_modf, b_pp], [b_modf])
                kb.ts("dve", qgs[:], P_("qg"), SCALE, None, ALU.mult, None, [b_pp], [b_modf])
                ones_f = sb(st, "ones_f", [128, 128], F32); b_of = Buf()
                kb.op("dve", lambda e: e.memset(ones_f[:], 1.0), writes=[b_of])
                dg = ring(st, "dg", [128, 128], F32, 2)
                for vi, dst in ((2, gab), (5, gfb)):
                    for kt in range(8):
                        d_t, d_b = dg.next()
                        kb.ts("dve", d_t[:], ident_f[:], modf[:, vi * 8 + kt:vi * 8 + kt + 1], None, ALU.mult, None,
                              [b_c, b_modf], [d_b])
                        kb.mm(psb[:, kt // 4, (kt % 4) * 128:(kt % 4 + 1) * 128], ones_f[:], d_t[:], True, True,
                              reads=[b_of, d_b], writes=[b_psb], signal=True)
                    kb.cp("dve", dst[:].rearrange("p (n f) -> p n f", n=2), psb[:], [b_psb], [b_gb])
                if debug:
                    kb.dma("sp", dout("d_modf", [128, 48]), modf[:], reads=[b_modf])
                    kb.dma("sp", dout("d_gab", [128, D]), gab[:], reads=[b_gb])


            if 3 in phases:
                ssm_tables(kb, nc, st, sb, pm, lam2, c2, b2, lam1, b1, klag, bp, cpt, ld, NLD, ident_f, ppt,
                           b_klag, b_bp, b_cpt, b_ld, b_pp, b_c, debug, dout, mid=mod_part)
            else:
                mod_part()
        kb.barrier()

        if 1 in phases:
            with ExitStack() as st:
                phase1(kb, nc, st, sb, pm, ring, locals())
            kb.barrier()

        if debug and 1 in phases and "dbg1" not in SKIP:
            with ExitStack() as st:
                kb.dma("sp", dout("d_U", [128, 4, S], BF16), dU, reads=bU)
                kb.dma("sp", dout("d_negm", [128, 4]), negm[:], reads=[b_negm])
                for nm, src, shp in (("d_QN", dQN, [128, 4, S]), ("d_QR", dQR, [128, 4, S]), ("d_KN", dKN, [128, 4, S]),
                                     ("d_KR", dKR, [64, S]), ("d_V", dV, [S // 128, 128, 512])):
                    kb.dma("sp", dout(nm, shp, BF16), src, reads=bQ + bK)
            kb.barrier()

        if 2 in phases:
            with ExitStack() as st:
                phase2(kb, nc, st, sb, pm, ring, locals())
            kb.barrier()
            if debug:
                kb.dma("sp", dout("d_AN", [128, 4, S], BF16), dAN, reads=bAN)
                kb.barrier()
        if 3 in phases:
            with ExitStack() as st:
                try:
                    phase3(kb, nc, st, sb, pm, ring, locals())
                except _Stop:
                    pass
            kb.barrier()
            if debug:
                kb.dma("sp", dout("d_SN", [128, 4, S], BF16), dSN, reads=bSN)
                kb.barrier()
        if 4 in phases:
            with ExitStack() as st:
                phase4(kb, nc, st, sb, pm, ring, locals())
            kb.barrier()

        kb.barrier(("sp",))
        kb.emit()
    return nc, dbg


def phase1(kb, nc, st, sb, pm, ring, E):
    g = E
    S, NT, NC = g["S"], g["NT"], g["NC"]
    x, pos = g["x"], g["pos"]
    ppt, P_ = g["ppt"], g["P_"]
    b_pp, b_c, b_modf = g["b_pp"], g["b_c"], g["b_modf"]
    ident_b, ones_b, selA, selB = g["ident_b"], g["ones_b"], g["selA"], g["selB"]
    modf, a1, qgs = g["modf"], g["a1"], g["qgs"]
    dU, bU = g["dU"], g["bU"]
    negm, b_negm = g["negm"], g["b_negm"]
    debug, dout = g["debug"], g["dout"]

    if "noalloc1" in SKIP:
        return
    if "bigalloc" in SKIP:
        import os
        n = int(os.environ.get("BIGKB", "100"))
        big = sb(st, "big", [128, n * 256], F32)
        print("sbuf remaining", nc.sbuf_bytes_remaining)
        return
    WIN = sb(st, "WIN", [128, 8, 1280], BF16)
    WUQ = sb(st, "WUQ", [128, 3, 1024], BF16)
    WUKN = sb(st, "WUKN", [128, 2, 512], BF16)
    WUV = sb(st, "WUV", [128, 2, 512], BF16)
    b_w = Buf()
    bWc = g["bWc"]
    kb.dma("sp", WIN[:], g["dWIN"].rearrange("(kt p) n -> p kt n", p=128), reads=[bWc], writes=[b_w])
    kb.dma("sp", WUQ[:], g["dWUQ"].rearrange("(kt p) n -> p kt n", p=128), reads=[bWc], writes=[b_w])
    kb.dma("sp", WUKN[:], g["dWUKN"].rearrange("(kt p) n -> p kt n", p=128), reads=[bWc], writes=[b_w])
    kb.dma("sp", WUV[:], g["dWUV"].rearrange("(kt p) n -> p kt n", p=128), reads=[bWc], writes=[b_w])

    if "alloc_a" in SKIP:
        return
    xs = ring(st, "xs", [128, D], F32, 3)
    posi = ring(st, "posi", [128, 512], I32, 2)
    junk = sb(st, "junk", [128, D], BF16); b_junk = Buf()
    ssq = ring(st, "ssq", [128, 8], F32, 2)
    xn = ring(st, "xn", [128, 4, D], BF16, 1)
    h1 = ring(st, "h1", [128, 8, 512], BF16, 2)
    zf = ring(st, "zf", [128, 512], F32, 6)
    sq = ring(st, "sq", [128, 512], BF16, 6)
    rs = ring(st, "rs", [128, 512], F32, 2)
    krsq = ring(st, "krsq", [64, 512], BF16, 1)
    zqn = ring(st, "zqn", [128, 3, 512], BF16, 2)
    zkvn = ring(st, "zkvn", [128, 2, 512], BF16, 2)
    rt = ring(st, "rt", [128, 512], F32, 6)
    c2t = ring(st, "c2t", [128, 512], F32, 2)
    s2t = ring(st, "s2t", [128, 512], F32, 2)
    QN = ring(st, "QN", [128, 4, 512], BF16, 2)
    QR = ring(st, "QR", [128, 4, 512], BF16, 2)
    for (qz_t, qz_b) in zip(QR.tiles, QR.bufs):
        kb.op("pool", lambda e, t=qz_t: e.memset(t[:], 0.0), writes=[qz_b])
    KN = ring(st, "KN", [128, 4, 512], BF16, 2)
    KR = ring(st, "KR", [64, 512], BF16, 2)
    VT = ring(st, "VT", [128, 4, 512], BF16, 2)
    UT = ring(st, "UT", [128, 4, 512], BF16, 2)
    qmax = sb(st, "qmax", [128, 4, NT], F32); b_qmax = Buf()
    kmax = sb(st, "kmax", [128, 4, NT], F32); b_kmax = Buf()

    if "alloc_b" in SKIP:
        return
    ptr = ring(st, "ptr", [128, 1024], BF16, 2, psum=True)
    pz = ring(st, "pz", [128, 512], F32, 4, psum=True)
    pn = ring(st, "pn", [128, 512], F32, 2, psum=True)

    if "alloc_c" in SKIP:
        return
    dQN, dQR, dKN, dKR, dV = g["dQN"], g["dQR"], g["dKN"], g["dKR"], g["dV"]
    bQ, bK = g["bQ"], g["bK"]

    def rstd_from_psum(pn_t, pn_b, n, eng="dve"):
        r_t, r_b = rs.next()
        kb.act(r_t[:], pn_t[:], AF.Sqrt, [pn_b, b_pp], [r_b], scale=1.0 / n, bias=P_("eps"))
        kb.op("dve", lambda e: e.reciprocal(out=r_t[:], in_=r_t[:]), [r_b], [r_b])
        return r_t, r_b

    fronts = {}

    fa = {}

    fr = {}

    def front_a(tt):
        t0 = tt * 512
        sq_t, sq_b = ssq.next()
        kb.op("dve", lambda e, t=sq_t: e.memset(t[:], 0.0), writes=[sq_b])
        xn_t, xn_b = xn.next()
        for s_ in range(4):
            x_t, x_b = xs.next()
            kb.dma("sp", x_t[:], x[t0 + s_ * 128:t0 + (s_ + 1) * 128, :], writes=[x_b])
            kb.act(junk[:], x_t[:], AF.Square, [x_b], [b_junk, sq_b], accum_out=sq_t[:, s_:s_ + 1])
            kb.act(sq_t[:, 4 + s_:5 + s_], sq_t[:, s_:s_ + 1], AF.Sqrt, [sq_b, b_pp], [sq_b], scale=1.0 / D, bias=P_("eps"))
            kb.op("dve", lambda e, t=sq_t, s_=s_: e.reciprocal(out=t[:, 4 + s_:5 + s_], in_=t[:, 4 + s_:5 + s_]), [sq_b], [sq_b])
            if s_ % 2 == 0:
                kb.ts("dve", xn_t[:, s_, :], x_t[:], sq_t[:, 4 + s_:5 + s_], None, ALU.mult, None, [x_b, sq_b], [xn_b])
            else:
                kb.act(xn_t[:, s_, :], x_t[:], AF.Copy, [x_b, sq_b], [xn_b], scale=sq_t[:, 4 + s_:5 + s_])
        fa[tt] = (xn_t, xn_b)

    def ropetab(tt):
        t0 = tt * 512
        pi_t, pi_b = posi.next()
        kb.dma("sp", pi_t[:], pos[:, t0:t0 + 512].partition_broadcast(128), writes=[pi_b])
        a_t, a_b = rt.next()
        k_t, k_b = rt.next()
        kb.cp("dve", a_t[:], pi_t[:], [pi_b], [a_b])
        kb.ts("dve", a_t[:], a_t[:], P_("invf"), None, ALU.mult, None, [a_b, b_pp], [a_b])
        kb.ts("dve", k_t[:], a_t[:], 1.0 / (2 * math.pi), MAGIC, ALU.mult, ALU.add, [a_b], [k_b])
        kb.ts("dve", k_t[:], k_t[:], -MAGIC, None, ALU.add, None, [k_b], [k_b])
        kb.stt("dve", a_t[:], k_t[:], -C1, a_t[:], ALU.mult, ALU.add, [k_b, a_b], [a_b])
        kb.stt("dve", a_t[:], k_t[:], -C2, a_t[:], ALU.mult, ALU.add, [k_b, a_b], [a_b])
        kb.stt("dve", a_t[:], k_t[:], -C3, a_t[:], ALU.mult, ALU.add, [k_b, a_b], [a_b])
        kb.ts("dve", a_t[:], a_t[:], 3.1415925, -3.1415925, ALU.min, ALU.max, [a_b], [a_b])
        kb.stt("dve", k_t[:], a_t[:], -1.0, a_t[:], ALU.mult, ALU.max, [a_b], [k_b])
        c_t, c_b = c2t.next()
        s_t, s_b = s2t.next()
        kb.act(c_t[:], k_t[:], AF.Sin, [k_b, b_pp], [c_b], scale=-1.0, bias=P_("halfpi"))
        kb.act(s_t[:], a_t[:], AF.Sin, [a_b, b_pp], [s_b], scale=P_("sgn"))
        fr[tt] = (c_t, c_b, s_t, s_b)

    def front_b(tt):
        xn_t, xn_b = fa.pop(tt)
        h_t, h_b = h1.next()
        for kt in range(8):
            p_t, p_b = ptr.next()
            for s_ in range(4):
                kb.tr(p_t[:, s_ * 128:(s_ + 1) * 128], xn_t[:, s_, kt * 128:(kt + 1) * 128], ident_b[:],
                      reads=[xn_b, b_c], writes=[p_b], signal=(s_ == 3))
            kb.act(h_t[:, kt, :], p_t[:, 0:512], AF.Identity, [p_b, b_modf], [h_b],
                   scale=a1[:, kt:kt + 1], bias=modf[:, kt:kt + 1])
        if debug and tt == 0:
            tmp = sb(st, "dbgh1", [128, 8, 512], F32); bt = Buf()
            kb.cp("dve", tmp[:], h_t[:], [h_b], [bt])
            kb.dma("sp", dout("d_h1", [128, 8, 512]), tmp[:], reads=[bt])
        fronts[tt] = (h_t, h_b)

    def tile_body(tt):
        t0 = tt * 512
        h_t, h_b = fronts.pop(tt)
        c_t, c_b, s_t, s_b = fr.pop(tt)

        def inproj(col0, M):
            z_t, z_b = pz.next()
            for kt in range(8):
                kb.mm(z_t[0:M, :], WIN[:, kt, col0:col0 + M], h_t[:, kt, :], kt == 0, kt == 7,
                      reads=[b_w, h_b], writes=[z_b], signal=(kt == 7))
            return z_t, z_b

        def normed_start(cols):
            lst = []
            for i, c0 in enumerate(cols):
                z_t, z_b = inproj(c0, 128)
                f_t, f_b = zf.next()
                kb.cp("act", f_t[:], z_t[:], [z_b], [f_b])
                q_t, q_b = sq.next()
                kb.tt("dve", q_t[:], f_t[:], f_t[:], ALU.mult, [f_b], [q_b])
                lst.append((f_t, f_b, q_t, q_b))
            return lst

        def normed_finish(lst, n, gcol, dst_t, dst_b):
            pn_t, pn_b = pn.next()
            for i, (f_t, f_b, q_t, q_b) in enumerate(lst):
                kb.mm(pn_t[:], ones_b[:], q_t[:], i == 0, i == len(lst) - 1, reads=[q_b, b_c], writes=[pn_b],
                      signal=(i == len(lst) - 1))
            r_t, r_b = rstd_from_psum(pn_t, pn_b, n)
            for i, (f_t, f_b, q_t, q_b) in enumerate(lst):
                kb.stt("dve", dst_t[:, i, :], f_t[:], gcol(i), r_t[:], ALU.mult, ALU.mult, [f_b, r_b, b_pp, b_modf], [dst_b])

        def rope(pa, pa_b, pb, pb_b, dst, dst_b, M):
            t1, t1b = rt.next()
            t2, t2b = rt.next()
            kb.tt("dve", t1[0:M, :], pa[0:M, :], c_t[0:M, :], ALU.mult, [pa_b, c_b], [t1b])
            kb.tt("dve", t2[0:M, :], pb[0:M, :], s_t[0:M, :], ALU.mult, [pb_b, s_b], [t2b])
            kb.tt("dve", dst, t1[0:M, :], t2[0:M, :], ALU.add, [t1b, t2b], [dst_b])

        lq = normed_start([0, 128, 256])
        lk = normed_start([384, 512])
        za, za_b = inproj(640, 64)
        zb, zb_b = inproj(704, 64)
        kr_t, kr_b = KR.next()
        rope(za, za_b, zb, zb_b, kr_t[:, :], kr_b, 64)
        u_t, u_b = UT.next()
        for ft in range(4):
            z_t, z_b = inproj(768 + ft * 128, 128)
            kb.cp("act", u_t[:, ft, :], z_t[:], [z_b], [u_b])
        kb.dma("sp", dU[:, :, t0:t0 + 512], u_t[:], reads=[u_b], writes=[bU[tt]])
        if tt + 1 < NT:
            front_a(tt + 1)
        zq_t, zq_b = zqn.next()
        normed_finish(lq, QL, lambda i: qgs[:, i:i + 1], zq_t, zq_b)
        zk_t, zk_b = zkvn.next()
        normed_finish(lk, KVL, lambda i: P_("kvg", i), zk_t, zk_b)
        if tt + 1 < NT:
            front_b(tt + 1)
        qn_t, qn_b = QN.next()
        qr_t, qr_b = QR.next()

        def qproj(mt):
            z_t, z_b = pz.next()
            for kt in range(3):
                kb.mm(z_t[:], WUQ[:, kt, mt * 128:(mt + 1) * 128], zq_t[:, kt, :], kt == 0, kt == 2,
                      reads=[b_w, zq_b], writes=[z_b], signal=(kt == 2))
            return z_t, z_b

        for h in range(4):
            z_t, z_b = qproj(h)
            kb.cp("act", qn_t[:, h, :], z_t[:], [z_b], [qn_b])
        for pr in range(2):
            pa, pa_b = qproj(4 + pr)
            pb, pb_b = qproj(6 + pr)
            t1, t1b = rt.next()
            t2, t2b = rt.next()
            kb.tt("dve", t1[:], pa[:], c_t[:], ALU.mult, [pa_b, c_b], [t1b])
            kb.tt("dve", t2[:], pb[:], s_t[:], ALU.mult, [pb_b, s_b], [t2b])
            kb.tt("dve", qr_t[0:64, 2 * pr, :], t1[0:64, :], t2[0:64, :], ALU.add, [t1b, t2b], [qr_b])
            kb.tt("dve", qr_t[64:128, 2 * pr + 1, :], t1[64:128, :], t2[64:128, :], ALU.add, [t1b, t2b], [qr_b])
        kn_t, kn_b = KN.next()
        for h in range(4):
            z_t, z_b = pz.next()
            for kt in range(2):
                kb.mm(z_t[:], WUKN[:, kt, h * 128:(h + 1) * 128], zk_t[:, kt, :], kt == 0, kt == 1,
                      reads=[b_w, zk_b], writes=[z_b], signal=(kt == 1))
            kb.cp("act", kn_t[:, h, :], z_t[:], [z_b], [kn_b])
        v_t, v_b = VT.next()
        for s_ in range(4):
            z_t, z_b = pz.next()
            for kt in range(2):
                kb.mm(z_t[:], zk_t[:, kt, s_ * 128:(s_ + 1) * 128], WUV[:, kt, :], kt == 0, kt == 1,
                      reads=[b_w, zk_b], writes=[z_b], signal=(kt == 1))
            kb.cp("act" if s_ % 2 else "dve", v_t[:, s_, :], z_t[:], [z_b], [v_b])
        for h in range(4):
            p_t, p_b = pn.next()
            q1, q1b = sq.next()
            kb.act(q1[:], qn_t[:, h, :], AF.Square, [qn_b], [q1b])
            q2, q2b = sq.next()
            kb.act(q2[:], qr_t[:, h, :], AF.Square, [qr_b], [q2b])
            kb.mm(p_t[:], ones_b[:], q1[:], True, False, reads=[q1b, b_c], writes=[p_b])
            kb.mm(p_t[:], ones_b[:], q2[:], False, True, reads=[q2b, b_c], writes=[p_b], signal=True)
            kb.op("dve", lambda e, h=h, p_t=p_t, tt=tt: e.reduce_max(out=qmax[:, h, tt:tt + 1], in_=p_t[:], axis=AX.X), [p_b], [b_qmax])
        k2, k2b = krsq.next()
        kb.act(k2[0:64, :], kr_t[:, :], AF.Square, [kr_b], [k2b])
        for h in range(4):
            p_t, p_b = pn.next()
            k1, k1b = sq.next()
            kb.act(k1[:], kn_t[:, h, :], AF.Square, [kn_b], [k1b])
            kb.mm(p_t[:], ones_b[:], k1[:], True, False, reads=[k1b, b_c], writes=[p_b])
            kb.mm(p_t[:], ones_b[0:64, :], k2[0:64, :], False, True, reads=[k2b, b_c], writes=[p_b], signal=True)
            kb.op("dve", lambda e, h=h, p_t=p_t, tt=tt: e.reduce_max(out=kmax[:, h, tt:tt + 1], in_=p_t[:], axis=AX.X), [p_b], [b_kmax])
        kb.dma("sp", dQN[:, :, t0:t0 + 512], qn_t[:], reads=[qn_b], writes=[bQ[tt]])
        kb.dma("sp", dQR[:, :, t0:t0 + 512], qr_t[:], reads=[qr_b], writes=[bQ[tt]])
        kb.dma("sp", dKN[:, :, t0:t0 + 512], kn_t[:], reads=[kn_b], writes=[bK[tt]])
        kb.dma("sp", dKR[:, t0:t0 + 512], kr_t[:], reads=[kr_b], writes=[bK[tt]])
        kb.dma("sp", dV[tt * 4:(tt + 1) * 4].rearrange("s p d -> p s d"), v_t[:], reads=[v_b], writes=[bK[tt]])
        if tt + 1 < NT:
            ropetab(tt + 1)

    front_a(0)
    ropetab(0)
    front_b(0)
    for tt in range(NT):
        tile_body(tt)
    mx = sb(st, "mx", [128, 12], F32); b_mx = Buf()
    kb.op("dve", lambda e: e.reduce_max(out=mx[:, 0:4], in_=qmax[:], axis=AX.X), [b_qmax], [b_mx])
    kb.op("dve", lambda e: e.reduce_max(out=mx[:, 4:8], in_=kmax[:], axis=AX.X), [b_kmax], [b_mx])
    kb.tt("dve", mx[:, 8:12], mx[:, 0:4], mx[:, 4:8], ALU.mult, [b_mx], [b_mx])
    kb.act(mx[:, 8:12], mx[:, 8:12], AF.Sqrt, [b_mx], [b_mx])
    kb.ts("dve", negm[:], mx[:, 8:12], -1.01, None, ALU.mult, None, [b_mx], [b_negm])


def phase2(kb, nc, st, sb, pm, ring, E):
    g = E
    S, NT = g["S"], g["NT"]
    ppt, P_ = g["ppt"], g["P_"]
    b_pp, b_c = g["b_pp"], g["b_c"]
    ident_b, ones_b, tri_b = g["ident_b"], g["ones_b"], g["tri_b"]
    negm, b_negm = g["negm"], g["b_negm"]
    dQN, dQR, dKN, dKR, dV, dAN = g["dQN"], g["dQR"], g["dKN"], g["dKR"], g["dV"], g["dAN"]
    bQ, bK, bAN = g["bQ"], g["bK"], g["bAN"]

    KNs = sb(st, "KNs", [128, 4, S], BF16)
    KRs = sb(st, "KRs", [128, S], BF16)
    Vs = sb(st, "Vs", [128, S // 128, 512], BF16)
    bKV = [Buf() for _ in range(NT)]
    kv_done = set()

    def load_kv(tt):
        if tt >= NT or tt in kv_done:
            return
        kv_done.add(tt)
        t0 = tt * 512
        kb.dma("sp", KNs[:, :, t0:t0 + 512], dKN[:, :, t0:t0 + 512], reads=[bK[tt]], writes=[bKV[tt]])
        kb.dma("sp", KRs[0:64, t0:t0 + 512], dKR[:, t0:t0 + 512], reads=[bK[tt]], writes=[bKV[tt]])
        kb.dma("sp", KRs[64:128, t0:t0 + 512], dKR[:, t0:t0 + 512], reads=[bK[tt]], writes=[bKV[tt]])
        kb.dma("sp", Vs[:, tt * 4:(tt + 1) * 4, :], dV[tt * 4:(tt + 1) * 4].rearrange("s p d -> p s d"),
               reads=[bK[tt]], writes=[bKV[tt]])

    Qn = ring(st, "Qn", [128, 4, 512], BF16, 2)
    Qr = ring(st, "Qr", [128, 4, 512], BF16, 2)
    PT = ring(st, "PT", [128, 512], BF16, 4)
    AFt = ring(st, "AFt", [128, 4, 512], F32, 2)
    rl = ring(st, "rl", [128, 512], F32, 1)
    lacc = ring(st, "lacc", [128, 512], F32, 2)
    lhi = ring(st, "lhi", [128, 512], BF16, 2)
    sqa = ring(st, "sqa", [128, 512], BF16, 1)
    rsa = ring(st, "rsa", [128, 512], F32, 1)
    AN = ring(st, "AN", [128, 4, 512], BF16, 1)
    psS = ring(st, "psS", [128, 512], F32, 3, psum=True)
    psO = ring(st, "psO", [128, 512], F32, 2, psum=True)
    psL = ring(st, "psL", [128, 512], F32, 2, psum=True)
    psN = ring(st, "psN", [128, 512], F32, 1, psum=True)

    deferred = []

    def flush():
        while deferred:
            deferred.pop(0)()

    for j in range(NT):
        t0 = j * 512
        qn_t, qn_b = Qn.next()
        qr_t, qr_b = Qr.next()
        load_kv(j)
        kb.dma("sp", qn_t[:], dQN[:, :, t0:t0 + 512], reads=[bQ[j]], writes=[qn_b])
        kb.dma("sp", qr_t[:], dQR[:, :, t0:t0 + 512], reads=[bQ[j]], writes=[qr_b])
        load_kv(j + 1)
        load_kv(j + 2)
        af_t, af_b = AFt.next()
        nk = 4 * (j + 1)
        for h in range(4):
            o_t, o_b = psO.next()
            l_t, l_b = psL.next()
            r0 = (h % 2) * 64
            pend = None

            la_t, la_b = lacc.next()
            lst = {"init": False}
            any_dve = (j >= 1)

            def stageC(kt, c0, p_t, p_b):
                i_ = kt - 4 * j
                use_dve = (kt % 2 == 1) and i_ < 0
                kb.mm(o_t[:, c0:512], Vs[:, kt, h * 128:(h + 1) * 128], p_t[:, c0:512], kt == 0, kt == nk - 1,
                      reads=[p_b, bKV[kt // 4]], writes=[o_b], signal=use_dve)
                if use_dve:
                    if not lst["init"]:
                        kb.cp("dve", la_t[:], p_t[:], [p_b], [la_b])
                        lst["init"] = True
                    else:
                        kb.tt("dve", la_t[:], la_t[:], p_t[:], ALU.add, [p_b, la_b], [la_b])
                else:
                    kb.mm(l_t[:, c0:512], ones_b[:], p_t[:, c0:512], kt == 0, (kt == nk - 1) and not any_dve,
                          reads=[p_b, b_c], writes=[o_b, l_b], signal=True)

            pend = []
            for kt in range(nk):
                i = kt - 4 * j
                c0 = 128 * i if i > 0 else 0
                s_t, s_b = psS.next()
                kb.mm(s_t[:, c0:512], KNs[:, h, kt * 128:(kt + 1) * 128], qn_t[:, h, c0:512], True, False,
                      reads=[bKV[kt // 4], qn_b], writes=[s_b], signal=False)
                kb.mm(s_t[:, c0:512], KRs[:, kt * 128:(kt + 1) * 128], qr_t[:, h, c0:512],
                      False, i < 0, reads=[bKV[kt // 4], qr_b], writes=[s_b], signal=(i < 0))
                if i >= 0:
                    kb.mm(s_t[:, c0:c0 + 128], ident_b[:], tri_b[:], False, True, reads=[b_c], writes=[s_b], signal=True)
                p_t, p_b = PT.next()
                kb.act(p_t[:, c0:512], s_t[:, c0:512], AF.Exp, [s_b, b_negm], [p_b], bias=negm[:, h:h + 1], scale=1.0)
                pend.append((kt, c0, p_t, p_b))
                if len(pend) > 2:
                    stageC(*pend.pop(0))
                if kt == 3:
                    flush()
            while pend:
                stageC(*pend.pop(0))
            def fin_head(h=h, o_t=o_t, o_b=o_b, l_t=l_t, l_b=l_b, la_t=la_t, la_b=la_b, any_dve=any_dve, af_t=af_t, af_b=af_b):
                if any_dve:
                    hi_t, hi_b = lhi.next()
                    lo_t, lo_b = lhi.next()
                    kb.cp("dve", hi_t[:], la_t[:], [la_b], [hi_b])
                    kb.tt("dve", la_t[:], la_t[:], hi_t[:], ALU.subtract, [la_b, hi_b], [la_b])
                    kb.cp("dve", lo_t[:], la_t[:], [la_b], [lo_b])
                    kb.mm(l_t[:], ones_b[:], hi_t[:], False, False, reads=[hi_b, b_c], writes=[l_b], signal=False)
                    kb.mm(l_t[:], ones_b[:], lo_t[:], False, True, reads=[lo_b, b_c], writes=[l_b], signal=True)
                r_t, r_b = rl.next()
                kb.op("dve", lambda e, r_t=r_t, l_t=l_t: e.reciprocal(out=r_t[:], in_=l_t[:]), [l_b], [r_b])
                kb.tt("dve", af_t[:, h, :], o_t[:], r_t[:], ALU.mult, [o_b, r_b], [af_b])

            deferred.append(fin_head)
        def fin_qtile(j=j, t0=t0, af_t=af_t, af_b=af_b):
            n_t, n_b = psN.next()
            for h in range(4):
                q_t, q_b = sqa.next()
                kb.tt("dve", q_t[:], af_t[:, h, :], af_t[:, h, :], ALU.mult, [af_b], [q_b])
                kb.mm(n_t[:], ones_b[:], q_t[:], h == 0, h == 3, reads=[q_b, b_c], writes=[n_b], signal=(h == 3))
            rs_t, rs_b = rsa.next()
            kb.act(rs_t[:], n_t[:], AF.Sqrt, [n_b, b_pp], [rs_b], scale=1.0 / 512, bias=P_("eps"))
            kb.op("dve", lambda e, rs_t=rs_t: e.reciprocal(out=rs_t[:], in_=rs_t[:]), [rs_b], [rs_b])
            an_t, an_b = AN.next()
            for h in range(4):
                kb.stt("dve", an_t[:, h, :], af_t[:, h, :], P_("ag", h), rs_t[:], ALU.mult, ALU.mult,
                       [af_b, rs_b, b_pp], [an_b])
            kb.dma("sp", dAN[:, :, t0:t0 + 512], an_t[:], reads=[an_b], writes=[bAN[j]])
        deferred.append(fin_qtile)
    flush()


def phase4(kb, nc, st, sb, pm, ring, E):
    g = E
    S, NT = g["S"], g["NT"]
    x, out = g["x"], g["out"]
    ppt, P_ = g["ppt"], g["P_"]
    b_pp, b_c, b_modf, b_gb = g["b_pp"], g["b_c"], g["b_modf"], g["b_gb"]
    ident_b = g["ident_b"]
    modf, a2, gab, gfb = g["modf"], g["a2"], g["gab"], g["gfb"]
    dAN, dSN, dWUP = g["dAN"], g["dSN"], g["dWUP"]
    bAN, bSN, bWUP = g["bAN"], g["bSN"], g["bWUP"]

    WOUT = sb(st, "WOUT", [128, 8, D], BF16)
    WDN = sb(st, "WDN", [128, NKF, D], BF16)
    b_w = Buf()
    kb.dma("sp", WOUT[:], g["dWOUT"].rearrange("(kt p) n -> p kt n", p=128), reads=[g["bWc"]], writes=[b_w])
    kb.dma("sp", WDN[:], g["dWDN"].rearrange("(kt p) n -> p kt n", p=128), reads=[g["bWc"]], writes=[b_w])
    fgb = sb(st, "fgb", [128, D], F32); b_fg = Buf()
    kb.dma("sp", fgb[:], g["fing"].partition_broadcast(128), writes=[b_fg])
    wup = ring(st, "wup", [128, 8, 512], BF16, 2)
    ans = ring(st, "ans", [128, 8, 512], BF16, 1)
    xt = ring(st, "xt", [128, D], F32, 2)
    x1r = [(sb(st, "x1_%d" % i, [128, 4, D], F32), [Buf() for _ in range(4)]) for i in range(2)]
    tmp = ring(st, "tmp", [128, D], F32, 2)
    tmph = ring(st, "tmph", [128, 512], F32, 2)
    junk = sb(st, "junk4", [128, D], BF16); b_junk = Buf()
    ssq = ring(st, "ssq4", [128, 4], F32, 4)
    xn2 = ring(st, "xn2", [128, 4, D], BF16, 1)
    h2 = ring(st, "h2", [128, 8, 512], BF16, 1)
    G = ring(st, "G", [128, 514], F32, 2)
    halo = sb(st, "halo", [128, NKF, 2], F32); b_halo = [Buf() for _ in range(NKF)]
    acc = ring(st, "acc", [128, 512], F32, 2)
    gl = ring(st, "gl", [128, 512], F32, 2)
    actb = sb(st, "actb", [128, NKF, 512], BF16); b_actb = Buf()
    ptr = ring(st, "ptr4", [128, 1024], BF16, 2, psum=True)
    pz = ring(st, "pz4", [128, 512], F32, 4, psum=True)
    psX = [pm(st, "psX%d" % i, [128, 512], F32) for i in range(2)]
    b_psX = [Buf(), Buf()]
    kb.op("pool", lambda e: e.memset(halo[:], 0.0), writes=b_halo)

    def rstd_col(src_ap, reads):
        q_t, q_b = ssq.next()
        kb.op("dve", lambda e: e.memset(q_t[:], 0.0), writes=[q_b])
        kb.act(junk[:], src_ap, AF.Square, reads, [b_junk, q_b], accum_out=q_t[:, 0:1])
        kb.act(q_t[:, 1:2], q_t[:, 0:1], AF.Sqrt, [q_b, b_pp], [q_b], scale=1.0 / D, bias=P_("eps"))
        kb.op("dve", lambda e: e.reciprocal(out=q_t[:, 2:3], in_=q_t[:, 1:2]), [q_b], [q_b])
        return q_t, q_b

    gab2 = gab[:].rearrange("p (n f) -> p n f", n=2)
    gfb2 = gfb[:].rearrange("p (n f) -> p n f", n=2)
    state = {}

    def H1(tt):
        t0 = tt * 512
        a_t, a_b = ans.next()
        kb.dma("sp", a_t[:, 0:4, :], dAN[:, :, t0:t0 + 512], reads=[bAN[tt]], writes=[a_b])
        kb.dma("sp", a_t[:, 4:8, :], dSN[:, :, t0:t0 + 512], reads=[bSN[tt]], writes=[a_b])
        x1_t, x1_bs = x1r[tt % 2]
        xn_t, xn_b = xn2.next()
        for s_ in range(4):
            x_t, x_b = xt.next()
            kb.dma("sp", x_t[:], x[t0 + s_ * 128:t0 + (s_ + 1) * 128, :], writes=[x_b])
            for n in range(2):
                for kt in range(8):
                    kb.mm(psX[n][:], a_t[:, kt, s_ * 128:(s_ + 1) * 128], WOUT[:, kt, n * 512:(n + 1) * 512],
                          kt == 0, kt == 7, reads=[a_b, b_w], writes=[b_psX[n]], signal=(kt == 7))
                m_t, m_b = tmph.next()
                kb.tt("dve", m_t[:], psX[n][:], gab2[:, n, :], ALU.mult, [b_psX[n], b_gb], [m_b])
                kb.tt("dve", x1_t[:, s_, n * 512:(n + 1) * 512], m_t[:], x_t[:, n * 512:(n + 1) * 512], ALU.add,
                      [m_b, x_b], [x1_bs[s_]])
            q_t, q_b = rstd_col(x1_t[:, s_, :], [x1_bs[s_]])
            kb.act(xn_t[:, s_, :], x1_t[:, s_, :], AF.Copy, [x1_bs[s_], q_b], [xn_b], scale=q_t[:, 2:3])
        state[tt] = dict(x1=(x1_t, x1_bs), xn=(xn_t, xn_b))

    def H2(tt):
        xn_t, xn_b = state[tt]["xn"]
        h_t, h_b = h2.next()
        for kt in range(8):
            p_t, p_b = ptr.next()
            for s_ in range(4):
                kb.tr(p_t[:, s_ * 128:(s_ + 1) * 128], xn_t[:, s_, kt * 128:(kt + 1) * 128], ident_b[:],
                      reads=[xn_b, b_c], writes=[p_b], signal=(s_ == 3))
            kb.act(h_t[:, kt, :], p_t[:, 0:512], AF.Identity, [p_b, b_modf], [h_b],
                   scale=a2[:, kt:kt + 1], bias=modf[:, 24 + kt:25 + kt])
        state[tt]["h2"] = (h_t, h_b)

    def M(tt):
        h_t, h_b = state[tt]["h2"]
        for c in range(NKF // 2):
            w_t, w_b = wup.next()
            kb.dma("sp", w_t[:], dWUP[:, c * 512:(c + 1) * 512].rearrange("(kt p) n -> p kt n", p=128),
                   reads=[bWUP], writes=[w_b])
            for q in range(2):
                kt = 2 * c + q
                pg, pg_b = pz.next()
                pv, pv_b = pz.next()
                for k8 in range(8):
                    kb.mm(pg[:], w_t[:, k8, q * 256:q * 256 + 128], h_t[:, k8, :], k8 == 0, k8 == 7,
                          reads=[w_b, h_b], writes=[pg_b], signal=(k8 == 7))
                for k8 in range(8):
                    kb.mm(pv[:], w_t[:, k8, q * 256 + 128:q * 256 + 256], h_t[:, k8, :], k8 == 0, k8 == 7,
                          reads=[w_b, h_b], writes=[pv_b], signal=(k8 == 7))
                g_t, g_b = G.next()
                kb.cp("act", g_t[:, 0:2], halo[:, kt, :], [b_halo[kt]], [g_b])
                kb.cp("act", g_t[:, 2:514], pg[:], [pg_b], [g_b])
                kb.cp("act", halo[:, kt, :], g_t[:, 512:514], [g_b], [b_halo[kt]])
                ac, ac_b = acc.next()
                cw = PP["convw"][0] + kt * 3
                kb.ts("dve", ac[:], g_t[:, 2:514], ppt[:, cw + 2:cw + 3], P_("convb", kt), ALU.mult, ALU.add,
                      [g_b, b_pp], [ac_b])
                kb.stt("dve", ac[:], g_t[:, 1:513], ppt[:, cw + 1:cw + 2], ac[:], ALU.mult, ALU.add, [g_b, ac_b, b_pp], [ac_b])
                kb.stt("dve", ac[:], g_t[:, 0:512], ppt[:, cw:cw + 1], ac[:], ALU.mult, ALU.add, [g_b, ac_b, b_pp], [ac_b])
                l_t, l_b = gl.next()
                kb.act(l_t[:], ac[:], AF.Gelu_apprx_tanh, [ac_b], [l_b])
                kb.tt("dve", actb[:, kt, :], l_t[:], pv[:], ALU.mult, [l_b, pv_b], [b_actb])

    def T_(tt):
        t0 = tt * 512
        x1_t, x1_bs = state[tt]["x1"]
        for s_ in range(4):
            m_t, m_b = tmp.next()
            for n in range(2):
                for kt in range(NKF):
                    kb.mm(psX[n][:], actb[:, kt, s_ * 128:(s_ + 1) * 128], WDN[:, kt, n * 512:(n + 1) * 512],
                          kt == 0, kt == NKF - 1, reads=[b_actb, b_w], writes=[b_psX[n]], signal=(kt == NKF - 1))
                kb.tt("dve", m_t[:, n * 512:(n + 1) * 512], psX[n][:], gfb2[:, n, :], ALU.mult, [b_psX[n], b_gb], [m_b])
            kb.tt("dve", m_t[:], m_t[:], x1_t[:, s_, :], ALU.add, [m_b, x1_bs[s_]], [m_b])
            q_t, q_b = rstd_col(m_t[:], [m_b])
            o_t, o_b = tmp.next()
            kb.stt("dve", o_t[:], m_t[:], q_t[:, 2:3], fgb[:], ALU.mult, ALU.mult, [m_b, q_b, b_fg], [o_b])
            kb.dma("sp", out[t0 + s_ * 128:t0 + (s_ + 1) * 128, :], o_t[:], reads=[o_b])
        del state[tt]

    H1(0)
    H2(0)
    for tt in range(NT):
        M(tt)
        if tt + 1 < NT:
            H1(tt + 1)
        T_(tt)
        if tt + 1 < NT:
            H2(tt + 1)


def ssm_tables(kb, nc, st, sb, pm, lam2, c2, b2, lam1, b1, klag, bp, cpt, ld, NLD, ident_f, ppt,
               b_klag, b_bp, b_cpt, b_ld, b_pp, b_c, debug, dout, mid=None):
    bs = Buf()
    E = "dve"

    def tt_(o, a, b, op):
        kb.tt(E, o, a, b, op, [bs], [bs])

    def ts_(o, a, s1, s2, op0, op1=None):
        kb.ts(E, o, a, s1, s2, op0, op1, [bs], [bs])

    def stt_(o, a, sc, b, op0, op1):
        kb.stt(E, o, a, sc, b, op0, op1, [bs], [bs])

    def sincos(ang, sn, cs, t0, t1, t2):
        ts_(t0, ang, 1.0 / (2 * math.pi), MAGIC, ALU.mult, ALU.add)
        ts_(t0, t0, -MAGIC, None, ALU.add)
        stt_(t1, t0, -C1, ang, ALU.mult, ALU.add)
        stt_(t1, t0, -C2, t1, ALU.mult, ALU.add)
        stt_(t1, t0, -C3, t1, ALU.mult, ALU.add)
        ts_(t1, t1, 0.5, None, ALU.mult)
        tt_(t2, t1, t1, ALU.mult)
        ts_(t0, t2, SINC[5], None, ALU.mult)
        for k in (4, 3, 2, 1, 0):
            stt_(t0, t0, SINC[k], t2, ALU.add, ALU.mult)
        stt_(sn, t0, 1.0, t1, ALU.add, ALU.mult)
        ts_(t0, t2, COSC[5], None, ALU.mult)
        for k in (4, 3, 2, 1, 0):
            stt_(t0, t0, COSC[k], t2, ALU.add, ALU.mult)
        ts_(cs, t0, 1.0, None, ALU.add)
        tt_(t0, sn, sn, ALU.mult)
        stt_(sn, sn, 2.0, cs, ALU.mult, ALU.mult)
        ts_(cs, t0, -2.0, 1.0, ALU.mult, ALU.add)

    def lam_calc(lre, lim, ldt, F, pfx):
        n = lambda k: sb(st, pfx + k, [128, F], F32)
        dt, lr, mag, ang, sn, cs = n("dt"), n("lr"), n("mag"), n("ang"), n("sn"), n("cs")
        t0, t1, t2 = n("t0"), n("t1"), n("t2")
        abre, abim, zre, zim = n("abre"), n("abim"), n("zre"), n("zim")
        kb.act(dt[:], ldt, AF.Exp, [bs], [bs])
        ts_(lr[:], lre, -1e-4, None, ALU.min)
        tt_(t0[:], lr[:], dt[:], ALU.mult)
        kb.act(mag[:], t0[:], AF.Exp, [bs], [bs])
        tt_(ang[:], lim, dt[:], ALU.mult)
        sincos(ang[:], sn[:], cs[:], t0[:], t1[:], t2[:])
        tt_(abre[:], mag[:], cs[:], ALU.mult)
        tt_(abim[:], mag[:], sn[:], ALU.mult)
        tt_(t0[:], lr[:], lr[:], ALU.mult)
        tt_(t1[:], lim, lim, ALU.mult)
        tt_(t0[:], t0[:], t1[:], ALU.add)
        kb.op(E, lambda e: e.reciprocal(out=t0[:], in_=t0[:]), [bs], [bs])
        ts_(t1[:], abre[:], -1.0, None, ALU.add)
        tt_(t2[:], t1[:], lr[:], ALU.mult)
        tt_(zre[:], abim[:], lim, ALU.mult)
        tt_(zre[:], zre[:], t2[:], ALU.add)
        tt_(zre[:], zre[:], t0[:], ALU.mult)
        tt_(t2[:], abim[:], lr[:], ALU.mult)
        tt_(zim[:], t1[:], lim, ALU.mult)
        tt_(zim[:], t2[:], zim[:], ALU.subtract)
        tt_(zim[:], zim[:], t0[:], ALU.mult)
        return abre, abim, zre, zim, (t0, t1, t2, sn, cs)

    def powers(abre, abim, F, npow, pfx, tmps):
        pre = sb(st, pfx + "pre", [128, npow, F], F32)
        pim = sb(st, pfx + "pim", [128, npow, F], F32)
        t0, t1 = tmps[0], tmps[1]
        kb.op(E, lambda e: e.memset(pre[:, 0, :], 1.0), [bs], [bs])
        kb.op(E, lambda e: e.memset(pim[:, 0, :], 0.0), [bs], [bs])
        for k in range(npow - 1):
            tt_(t0[:], pre[:, k, :], abre[:], ALU.mult)
            tt_(t1[:], pim[:, k, :], abim[:], ALU.mult)
            tt_(pre[:, k + 1, :], t0[:], t1[:], ALU.subtract)
            tt_(t0[:], pre[:, k, :], abim[:], ALU.mult)
            tt_(t1[:], pim[:, k, :], abre[:], ALU.mult)
            tt_(pim[:, k + 1, :], t0[:], t1[:], ALU.add)
        return pre, pim

    l2 = sb(st, "l2", [128, 3, 16], F32)
    c2t = sb(st, "c2t_", [128, 2, 16, 32], F32)
    b2t = sb(st, "b2t", [128, 2, 16, 128], F32)
    kb.dma("sp", l2[:], lam2, writes=[bs])
    kb.dma("sp", c2t[:], c2, writes=[bs])
    kb.dma("sp", b2t[:], b2, writes=[bs])
    abre, abim, zre, zim, tm = lam_calc(l2[:, 0, :], l2[:, 1, :], l2[:, 2, :], 16, "a2")
    pre, pim = powers(abre, abim, 16, T + 1, "a2", tm)
    kb.cp(E, ld[:, :, 0, 0], pre[:, T, :], [bs], [bs, b_ld])
    kb.cp(E, ld[:, :, 0, 1], pim[:, T, :], [bs], [bs, b_ld])
    for k in range(NLD):
        tt_(tm[0][:], ld[:, :, k, 0], ld[:, :, k, 0], ALU.mult)
        tt_(tm[1][:], ld[:, :, k, 1], ld[:, :, k, 1], ALU.mult)
        tt_(ld[:, :, k + 1, 0], tm[0][:], tm[1][:], ALU.subtract)
        stt_(ld[:, :, k + 1, 1], ld[:, :, k, 0], 2.0, ld[:, :, k, 1], ALU.mult, ALU.mult)
    kb.ts(E, ld[:, :, :, 2], ld[:, :, :, 1], -1.0, None, ALU.mult, None, [bs], [bs, b_ld])
    bst = sb(st, "bst", [128, 16, 2, 128], BF16)
    w1 = sb(st, "w1", [128, 16, 128], F32)
    w2 = sb(st, "w2", [128, 16, 128], F32)
    zre_b = zre[:, :].unsqueeze(2).to_broadcast([128, 16, 128])
    zim_b = zim[:, :].unsqueeze(2).to_broadcast([128, 16, 128])
    tt_(w1[:], b2t[:, 0, :, :], zre_b, ALU.mult)
    tt_(w2[:], b2t[:, 1, :, :], zim_b, ALU.mult)
    tt_(bst[:, :, 0, :], w1[:], w2[:], ALU.subtract)
    tt_(w1[:], b2t[:, 0, :, :], zim_b, ALU.mult)
    tt_(w2[:], b2t[:, 1, :, :], zre_b, ALU.mult)
    tt_(bst[:, :, 1, :], w1[:], w2[:], ALU.add)
    cps = sb(st, "cps", [128, 16, T + 1, 2, 32], BF16)
    v1 = sb(st, "v1", [128, 16, 32], F32)
    v2 = sb(st, "v2", [128, 16, 32], F32)
    for k in range(T + 1):
        pr_b = pre[:, k, :].unsqueeze(2).to_broadcast([128, 16, 32])
        pi_b = pim[:, k, :].unsqueeze(2).to_broadcast([128, 16, 32])
        tt_(v1[:], c2t[:, 0, :, :], pr_b, ALU.mult)
        tt_(v2[:], c2t[:, 1, :, :], pi_b, ALU.mult)
        tt_(cps[:, :, k, 0, :], v1[:], v2[:], ALU.subtract)
        tt_(v1[:], c2t[:, 0, :, :], pi_b, ALU.mult)
        tt_(v2[:], c2t[:, 1, :, :], pr_b, ALU.mult)
        stt_(cps[:, :, k, 1, :], v1[:], -1.0, v2[:], ALU.mult, ALU.subtract)
    kb.dma("sp", cpt, cps[:], reads=[bs], writes=[b_cpt])
    l1 = sb(st, "l1", [128, 3, 256], F32)
    b1t = sb(st, "b1t", [128, 2, 4, 128], F32)
    kb.dma("sp", l1[:], lam1.rearrange("p a f q -> p a (f q)"), writes=[bs])
    kb.dma("sp", b1t[:], b1, writes=[bs])
    abre1, abim1, zre1, zim1, tm1 = lam_calc(l1[:, 0, :], l1[:, 1, :], l1[:, 2, :], 256, "a1")
    cur_re = sb(st, "cur_re", [128, 256], F32)
    cur_im = sb(st, "cur_im", [128, 256], F32)
    nxt_re = sb(st, "nxt_re", [128, 256], F32)
    kb.op(E, lambda e: e.memset(cur_re[:], 1.0), [bs], [bs])
    kb.op(E, lambda e: e.memset(cur_im[:], 0.0), [bs], [bs])
    bbre = sb(st, "bbre", [128, 4, 2, 64], F32)
    bbim = sb(st, "bbim", [128, 4, 2, 64], F32)
    x1_ = sb(st, "x1_", [128, 4, 2, 64], F32)
    x2_ = sb(st, "x2_", [128, 4, 2, 64], F32)

    def bq(t):
        return t.rearrange("p (f q) -> p f q", f=4).unsqueeze(2).to_broadcast([128, 4, 2, 64])

    def v4(t):
        return t.rearrange("p f (t q) -> p f t q", t=2)

    tt_(x1_[:], v4(b1t[:, 0, :, :]), bq(zre1[:, :]), ALU.mult)
    tt_(x2_[:], v4(b1t[:, 1, :, :]), bq(zim1[:, :]), ALU.mult)
    tt_(bbre[:], x1_[:], x2_[:], ALU.subtract)
    tt_(x1_[:], v4(b1t[:, 0, :, :]), bq(zim1[:, :]), ALU.mult)
    tt_(x2_[:], v4(b1t[:, 1, :, :]), bq(zre1[:, :]), ALU.mult)
    tt_(bbim[:], x1_[:], x2_[:], ALU.add)
    bps = sb(st, "bps", [128, 4, T, 2, 128], BF16)
    for i in range(T):
        if i > 0:
            t0_, t1_ = tm1[0], tm1[1]
            tt_(t0_[:], cur_re[:], abre1[:], ALU.mult)
            tt_(t1_[:], cur_im[:], abim1[:], ALU.mult)
            tt_(nxt_re[:], t0_[:], t1_[:], ALU.subtract)
            tt_(t0_[:], cur_re[:], abim1[:], ALU.mult)
            tt_(t1_[:], cur_im[:], abre1[:], ALU.mult)
            tt_(cur_im[:], t0_[:], t1_[:], ALU.add)
            kb.cp(E, cur_re[:], nxt_re[:], [bs], [bs])
        pr_b = bq(cur_re[:, :])
        pi_b = bq(cur_im[:, :])
        tt_(x1_[:], bbre[:], pr_b, ALU.mult)
        tt_(x2_[:], bbim[:], pi_b, ALU.mult)
        tt_(v4(bps[:, :, i, 0, :]), x1_[:], x2_[:], ALU.subtract)
        tt_(x1_[:], bbim[:], pr_b, ALU.mult)
        tt_(x2_[:], bbre[:], pi_b, ALU.mult)
        tt_(v4(bps[:, :, i, 1, :]), x1_[:], x2_[:], ALU.add)
    kb.dma("sp", bp, bps[:], reads=[bs], writes=[b_bp])

    if mid is not None:
        mid()
    psK = pm(st, "psK", [128, T, 128], F32); b_psK = Buf()
    kls = sb(st, "kls", [128, T, 128], BF16); b_kls = Buf()
    for ft in range(4):
        for tau in range(T):
            for pr in range(4):
                pair = 4 * ft + pr
                kb.mm(psK[:, tau, 32 * pr:32 * pr + 32], bst[:, pair, 0, :], cps[:, pair, tau, 0, :], True, False,
                      reads=[bs], writes=[b_psK], signal=False)
                kb.mm(psK[:, tau, 32 * pr:32 * pr + 32], bst[:, pair, 1, :], cps[:, pair, tau, 1, :], False, True,
                      reads=[bs], writes=[b_psK], signal=(pr == 3 and tau == T - 1))
        kb.cp("dve", kls[:, 1:T, :], psK[:, 1:T, :], [b_psK], [b_kls])
        a_, _ = PP["ssmd"]
        kb.stt("dve", kls[:, 0, :], ident_f[:], ppt[:, a_ + ft:a_ + ft + 1], psK[:, 0, :], ALU.mult, ALU.add,
               [b_psK, b_c, b_pp], [b_kls, b_psK])
        kb.dma("sp", klag[:, ft], kls[:], reads=[b_kls], writes=[b_klag])


def phase3(kb, nc, st, sb, pm, ring, E):
    g = E
    S, NT, NC, NLD = g["S"], g["NT"], g["NC"], g["NLD"]
    ppt, P_ = g["ppt"], g["P_"]
    b_pp, b_c = g["b_pp"], g["b_c"]
    ones_b = g["ones_b"]
    klag, bp, cpt, ld = g["klag"], g["bp"], g["cpt"], g["ld"]
    b_klag, b_bp, b_cpt, b_ld = g["b_klag"], g["b_bp"], g["b_cpt"], g["b_ld"]
    dU, bU, dSN, bSN = g["dU"], g["bU"], g["dSN"], g["bSN"]

    WGLU = sb(st, "WGLU", [128, 4, 1024], BF16); b_w = Buf()
    kb.dma("sp", WGLU[:], g["dWGLU"].rearrange("(kt p) n -> p kt n", p=128), reads=[g["bWc"]], writes=[b_w])
    Gn = sb(st, "Gn", [128, 4, S], BF16); b_gn = [Buf() for _ in range(4)]
    psV = ring(st, "psV", [128, 512], F32, 2, psum=True)
    psY = ring(st, "psY", [128, 512], F32, 2, psum=True)
    pz = ring(st, "pz3", [128, 512], F32, 3, psum=True)
    psN = ring(st, "psN3", [128, 512], F32, 1, psum=True)

    with ExitStack() as st2:
        KL = ring(st2, "KL", [128, T, 128], BF16, 2)
        BPs = ring(st2, "BPs", [128, T, 2, 128], BF16, 2)
        CPs = ring(st2, "CPs", [128, 4, T + 1, 2, 32], BF16, 2)
        Unat = ring(st2, "Unat", [128, S], BF16, 1)
        Ujm = ring(st2, "Ujm", [128, T, NC], BF16, 2)
        S16 = ring(st2, "S16", [128, 4, 2, NC + 1], BF16, 2)
        scA = [ring(st2, "scA%d" % i, [128, 2, NC], F32, 1) for i in range(2)]
        scB = [ring(st2, "scB%d" % i, [128, 2, NC], F32, 1) for i in range(2)]

        def load(ft):
            c = {"ft": ft}
            c["kl"] = KL.next(); c["bp"] = BPs.next(); c["cp"] = CPs.next()
            un_t, un_b = Unat.next()
            c["uj"] = Ujm.next(); c["s16"] = S16.next()
            kb.dma("sp", c["kl"][0][:], klag[:, ft], reads=[b_klag], writes=[c["kl"][1]])
            kb.dma("sp", c["bp"][0][:], bp[:, ft], reads=[b_bp], writes=[c["bp"][1]])
            kb.dma("sp", c["cp"][0][:], cpt[:, 4 * ft:4 * ft + 4], reads=[b_cpt], writes=[c["cp"][1]])
            kb.dma("sp", un_t[:], dU[:, ft, :], reads=bU, writes=[un_b])
            kb.cp("dve", c["uj"][0][:], un_t[:, :].rearrange("p (c j) -> p j c", j=T), [un_b], [c["uj"][1]])
            kb.op("dve", lambda e, t=c["s16"][0]: e.memset(t[:, :, :, 0:1], 0.0), writes=[c["s16"][1]])
            return c

        def scan_gen(c):
            ft = c["ft"]
            bp_t, bp_b = c["bp"]; uj_t, uj_b = c["uj"]; s16_t, s16_b = c["s16"]
            for pr0 in (0, 2):
                bufs = []
                for pr in (pr0, pr0 + 1):
                    a_t, a_b = scA[pr % 2].next()
                    b_t, b_b = scB[pr % 2].next()
                    for ri in range(2):
                        v_t, v_b = psV.next()
                        for i in range(T):
                            kb.mm(v_t[:, 0:NC], bp_t[32 * pr:32 * pr + 32, i, ri, :],
                                  uj_t[32 * pr:32 * pr + 32, T - 1 - i, :], i == 0, i == T - 1,
                                  reads=[bp_b, uj_b], writes=[v_b], signal=(i == T - 1), tp=(32 * pr, 0))
                        kb.cp("act", a_t[:, ri, :], v_t[:, 0:NC], [v_b], [a_b])
                    bufs.append([pr, a_t, a_b, b_t, b_b])
                yield
                for k in range(NLD):
                    d = 1 << k
                    n = NC
                    for step in range(5):
                        for bf in bufs:
                            pr, src, src_b, dst, dst_b = bf
                            pair = 4 * ft + pr
                            lre = ld[:, pair, k, 0:1]
                            lim = ld[:, pair, k, 1:2]
                            nlim = ld[:, pair, k, 2:3]
                            if step == 0:
                                kb.stt("dve", dst[:, 0, d:n], src[:, 1, 0:n - d], nlim, src[:, 0, d:n], ALU.mult, ALU.add,
                                       [src_b, b_ld], [dst_b])
                            elif step == 1:
                                kb.stt("dve", dst[:, 0, d:n], src[:, 0, 0:n - d], lre, dst[:, 0, d:n], ALU.mult, ALU.add,
                                       [src_b, b_ld, dst_b], [dst_b])
                            elif step == 2:
                                kb.stt("dve", dst[:, 1, d:n], src[:, 0, 0:n - d], lim, src[:, 1, d:n], ALU.mult, ALU.add,
                                       [src_b, b_ld, dst_b], [dst_b])
                            elif step == 3:
                                kb.stt("dve", dst[:, 1, d:n], src[:, 1, 0:n - d], lre, dst[:, 1, d:n], ALU.mult, ALU.add,
                                       [src_b, b_ld, dst_b], [dst_b])
                            else:
                                kb.cp("act", dst[:, :, 0:d], src[:, :, 0:d], [src_b, dst_b], [dst_b])
                    for bf in bufs:
                        bf[1], bf[2], bf[3], bf[4] = bf[3], bf[4], bf[1], bf[2]
                    yield
                for bf in bufs:
                    pr, src, src_b, dst, dst_b = bf
                    kb.cp("act", s16_t[:, pr, :, 1:NC + 1], src[:, :, :], [src_b], [s16_b])
                yield

        def ad_gen(c):
            ft = c["ft"]
            kl_t, kl_b = c["kl"]; cp_t, cp_b = c["cp"]; uj_t, uj_b = c["uj"]; s16_t, s16_b = c["s16"]
            gview = Gn[:, ft, :].rearrange("p (c j) -> p j c", j=T)
            for j in range(T):
                y_t, y_b = psY.next()
                for tau in range(j + 1):
                    kb.mm(y_t[:, 0:NC], kl_t[:, tau, :], uj_t[:, j - tau, :], tau == 0, False,
                          reads=[kl_b, uj_b], writes=[y_b], signal=False)
                for pr in range(4):
                    for ri in range(2):
                        last = (pr == 3 and ri == 1)
                        kb.mm(y_t[32 * pr:32 * pr + 32, 0:NC], cp_t[:, pr, j + 1, ri, :], s16_t[:, pr, ri, 0:NC], False, last,
                              reads=[cp_b, s16_b], writes=[y_b], signal=last, tp=(0, 32 * pr))
                kb.act(gview[:, j, :], y_t[:, 0:NC], AF.Gelu_apprx_tanh, [y_b], [b_gn[ft]])
                yield

        def interleave(*gs):
            gens = [x for x in gs if x is not None]
            while gens:
                for x in list(gens):
                    try:
                        next(x)
                    except StopIteration:
                        gens.remove(x)

        prev = None
        for ft in range(4):
            c = load(ft)
            interleave(scan_gen(c), ad_gen(prev) if prev is not None else None)
            prev = c
        interleave(ad_gen(prev))

    sgm = ring(st, "sgm", [128, 512], F32, 2)
    sf = ring(st, "sf", [128, 4, 512], F32, 1)
    sq3 = ring(st, "sq3", [128, 512], BF16, 2)
    rs3 = ring(st, "rs3", [128, 512], F32, 1)
    sn = ring(st, "sn", [128, 4, 512], BF16, 2)
    for tt in range(NT):
        t0 = tt * 512
        f_t, f_b = sf.next()
        for m in range(4):
            pa, pa_b = pz.next()
            pb, pb_b = pz.next()
            for k in range(4):
                kb.mm(pa[:], WGLU[:, k, m * 128:(m + 1) * 128], Gn[:, k, t0:t0 + 512], k == 0, k == 3,
                      reads=[b_w] + b_gn, writes=[pa_b], signal=(k == 3))
            for k in range(4):
                kb.mm(pb[:], WGLU[:, k, 512 + m * 128:512 + (m + 1) * 128], Gn[:, k, t0:t0 + 512], k == 0, k == 3,
                      reads=[b_w] + b_gn, writes=[pb_b], signal=(k == 3))
            g_t, g_b = sgm.next()
            kb.act(g_t[:], pb[:], AF.Sigmoid, [pb_b, b_pp], [g_b], bias=P_("bglu", 4 + m))
            kb.stt("dve", f_t[:, m, :], pa[:], P_("bglu", m), g_t[:], ALU.add, ALU.mult, [pa_b, g_b, b_pp], [f_b])
        n_t, n_b = psN.next()
        for m in range(4):
            q_t, q_b = sq3.next()
            kb.tt("dve", q_t[:], f_t[:, m, :], f_t[:, m, :], ALU.mult, [f_b], [q_b])
            kb.mm(n_t[:], ones_b[:], q_t[:], m == 0, m == 3, reads=[q_b, b_c], writes=[n_b], signal=(m == 3))
        r_t, r_b = rs3.next()
        kb.act(r_t[:], n_t[:], AF.Sqrt, [n_b, b_pp], [r_b], scale=1.0 / 512, bias=P_("eps"))
        kb.op("dve", lambda e, r_t=r_t: e.reciprocal(out=r_t[:], in_=r_t[:]), [r_b], [r_b])
        s_t, s_b = sn.next()
        for m in range(4):
            kb.stt("dve", s_t[:, m, :], f_t[:, m, :], P_("sg", m), r_t[:], ALU.mult, ALU.mult, [f_b, r_b, b_pp], [s_b])
        kb.dma("sp", dSN[:, :, t0:t0 + 512], s_t[:], reads=[s_b], writes=[bSN[tt]])


def _pm(v, n):
    return np.ascontiguousarray(np.asarray(v, np.float32).reshape(n, 128).T)


def prep_shared(inp):
    f = lambda k: np.asarray(inp[k], np.float32)
    m = {}
    pp = np.zeros((128, NPP), np.float32)

    def put(name, arr):
        a, b = PP[name]
        pp[:, a:b] = arr.reshape(128, b - a)

    put("bmod", _pm(f("b_mod")[0], 48))
    put("ln1", _pm(f("ln1_g")[0], 8))
    put("ln2", _pm(f("ln2_g")[0], 8))
    put("qg", _pm(f("q_norm_g")[0], 3))
    put("kvg", _pm(f("kv_norm_g")[0], 2))
    put("ag", _pm(f("attn_out_g")[0], 4))
    put("sg", _pm(f("ssm_out_g")[0], 4))
    put("bglu", _pm(f("b_glu")[0], 8))
    put("convw", np.ascontiguousarray(f("conv_w")[0].reshape(3, NKF, 128).transpose(2, 1, 0)).reshape(128, 66))
    put("convb", _pm(f("conv_b")[0], NKF))
    inv_freq = (np.float32(10000.0) ** (-np.arange(0, 64, 2, dtype=np.float32) / np.float32(64))).astype(np.float32)
    p = np.arange(128)
    put("invf", inv_freq[p % 32].reshape(128, 1))
    put("sgn", np.where((p % 64) < 32, -1.0, 1.0).astype(np.float32).reshape(128, 1))
    put("ssmd", np.ascontiguousarray(f("ssm_d")[0].reshape(4, 128).T))
    put("halfpi", np.full((128, 1), math.pi / 2, np.float32))
    put("eps", np.full((128, 1), EPS, np.float32))
    m["pp"] = pp
    m["bmod_row"] = f("b_mod")[0].reshape(1, 6 * D)
    m["fing"] = f("final_g").reshape(1, D)
    m["w_mod"] = f("w_mod")[0]
    w_in = f("w_in")[0]
    sw = np.concatenate([np.arange(672, 704), np.arange(640, 672)])
    m["w_in"] = np.ascontiguousarray(np.concatenate([w_in[:, :704], w_in[:, sw], w_in[:, 704:]], 1))
    w_uq = f("w_uq")[0]
    cols = []
    for h in range(4):
        cols.append(np.arange(h * 192, h * 192 + 128))
    for h in range(4):
        cols.append(np.arange(h * 192 + 128, h * 192 + 192))
    for h in range(4):
        cols.append(np.concatenate([np.arange(h * 192 + 160, h * 192 + 192), np.arange(h * 192 + 128, h * 192 + 160)]))
    m["w_uq"] = np.ascontiguousarray(w_uq[:, np.concatenate(cols)])
    w_ukv = f("w_ukv")[0]
    m["w_ukn"] = np.ascontiguousarray(np.concatenate([w_ukv[:, h * 256:h * 256 + 128] for h in range(4)], 1))
    m["w_uv"] = np.ascontiguousarray(np.concatenate([w_ukv[:, h * 256 + 128:h * 256 + 256] for h in range(4)], 1))
    m["w_glu"] = f("w_glu")[0]
    m["w_out"] = f("w_out")[0]
    w_up = f("w_up")[0]
    il = []
    for kt in range(NKF):
        il.append(np.arange(kt * 128, kt * 128 + 128))
        il.append(np.arange(DFF + kt * 128, DFF + kt * 128 + 128))
    m["w_up"] = np.ascontiguousarray(w_up[:, np.concatenate(il)])
    m["w_down"] = f("w_down")[0]
    lre, lim, ldt = f("ssm_lam_re")[0], f("ssm_lam_im")[0], f("ssm_log_dt")[0]
    bre, bim = f("ssm_b_re")[0], f("ssm_b_im")[0]
    cre, cim = f("ssm_c_re")[0], f("ssm_c_im")[0]
    lam2 = np.zeros((128, 3, 16), np.float32)
    c2 = np.zeros((128, 2, 16, 32), np.float32)
    b2 = np.zeros((128, 2, 16, 128), np.float32)
    for pair in range(16):
        for two in range(2):
            g_ = 2 * pair + two
            rows = slice(two * 64, two * 64 + 64)
            lam2[rows, 0, pair] = lre[g_]
            lam2[rows, 1, pair] = lim[g_]
            lam2[rows, 2, pair] = ldt[g_]
            c2[rows, 0, pair, two * 16:(two + 1) * 16] = cre[g_].T
            c2[rows, 1, pair, two * 16:(two + 1) * 16] = cim[g_].T
            c0 = 32 * (pair % 4) + 16 * two
            b2[rows, 0, pair, c0:c0 + 16] = bre[g_]
            b2[rows, 1, pair, c0:c0 + 16] = bim[g_]
    lam1 = np.zeros((128, 3, 4, 64), np.float32)
    b1 = np.zeros((128, 2, 4, 128), np.float32)
    for ft in range(4):
        for g8 in range(8):
            g_ = ft * 8 + g8
            rows = slice(g8 * 16, g8 * 16 + 16)
            two = g8 % 2
            lam1[rows, 0, ft, :] = lre[g_][None, :]
            lam1[rows, 1, ft, :] = lim[g_][None, :]
            lam1[rows, 2, ft, :] = ldt[g_]
            b1[rows, 0, ft, two * 64:(two + 1) * 64] = bre[g_].T
            b1[rows, 1, ft, two * 64:(two + 1) * 64] = bim[g_].T
    m["lam2"], m["c2"], m["b2"], m["lam1"], m["b1"] = lam2, c2, b2, lam1, b1
    cst = np.zeros((128, 4, 128), np.float32)
    cst[:, 0, :] = np.eye(128, dtype=np.float32)
    pi_, ci_ = np.meshgrid(np.arange(128), np.arange(128), indexing="ij")
    cst[:, 1, :] = np.where(pi_ <= ci_, 0.0, -30000.0)
    cst[0:64, 2, :] = 1.0
    cst[64:128, 3, :] = 1.0
    m["cst"] = cst
    return m


def prep_core(inp, b, S, shared):
    m = dict(shared)
    m["x"] = np.ascontiguousarray(np.asarray(inp["x"], np.float32)[b, :S])
    m["pos"] = np.ascontiguousarray(np.asarray(inp["positions"]).astype(np.int32)[b:b + 1, :S])
    m["cpm"] = _pm(np.asarray(inp["c"], np.float32)[b], 8)
    return m


_NC_CACHE = {}


def kernel(**inputs):
    S = inputs["x"].shape[1]
    B = inputs["x"].shape[0]
    if S not in _NC_CACHE:
        _NC_CACHE[S] = build(S)[0]
    nc = _NC_CACHE[S]
    shared = prep_shared(inputs)
    in_maps = [prep_core(inputs, b, S, shared) for b in range(B)]
    res = run_bass_kernel_spmd(nc, in_maps, core_ids=list(range(B)))
    return np.stack([np.asarray(r["out"], np.float32) for r in res.results], 0)
```

```python
import math
from contextlib import ExitStack

import numpy as np
import concourse.bass as bass
import concourse.mybir as mybir
from concourse.bass_utils import run_bass_kernel_spmd

F32 = mybir.dt.float32
BF16 = mybir.dt.bfloat16
I32 = mybir.dt.int32
ALU = mybir.AluOpType
AF = mybir.ActivationFunctionType
AX = mybir.AxisListType

D = 1024
NH = 4
QL = 384
KVL = 256
DSSM = 512
DFF = 2816
NKF = DFF // 128
EPS = 1e-6
T = 16
SCALE = (128 + 64) ** -0.5
MAGIC = 12582912.0
C1 = 6.28125
C2 = 0.0019353071693331003
C3 = 1.0253131677018246e-11
SINC = [-0.16666666666666666, 0.008333333333333333, -0.0001984126984126984,
        2.7557319223985893e-06, -2.505210838544172e-08, 1.6059043836821613e-10]
COSC = [-0.5, 0.041666666666666664, -0.001388888888888889, 2.48015873015873e-05,
        -2.755731922398589e-07, 2.08767569878681e-09]

ENGS = ("pe", "act", "dve", "pool", "sp")
SKIP = set()
STOP = 0


class _Stop(Exception):
    pass


def stage(n):
    if STOP and n > STOP:
        raise _Stop()

PP = {}
_o = 0
for _n, _w in (("bmod", 48), ("ln1", 8), ("ln2", 8), ("qg", 3), ("kvg", 2), ("ag", 4), ("sg", 4),
               ("bglu", 8), ("convw", 66), ("convb", 22), ("invf", 1), ("sgn", 1), ("ssmd", 4),
               ("halfpi", 1), ("eps", 1)):
    PP[_n] = (_o, _o + _w)
    _o += _w
NPP = _o


class Buf:
    __slots__ = ("name", "w", "r")

    def __init__(self, name=""):
        self.name = name
        self.w = []
        self.r = []


class KB:
    def __init__(self, nc, stack):
        self.nc = nc
        self.stack = stack
        self.streams = {e: [] for e in ENGS}
        self.sems = {}
        self.cnt = {}
        self.seen = {e: {} for e in ENGS}
        for e in ENGS:
            self._mksem("E_" + e)
        self.ndma = 0
        self.nswdma = 0
        self.pending = {}
        self.pending_idx = {}
        self.pend_r = {e: set() for e in ENGS}
        self.pend_w = {e: set() for e in ENGS}
        self.NDSEM = 24

    def _mksem(self, key):
        if key not in self.sems:
            self.sems[key] = self.stack.enter_context(self.nc.semaphore(key))
            self.cnt[key] = 0
        return self.sems[key]

    def _waits(self, eng, reads, writes):
        need = {}
        for b in reads:
            for (k, v) in b.w:
                if need.get(k, 0) < v:
                    need[k] = v
        for b in writes:
            for (k, v) in b.w:
                if need.get(k, 0) < v:
                    need[k] = v
            for (k, v) in b.r:
                if need.get(k, 0) < v:
                    need[k] = v
        out = []
        seen = self.seen[eng]
        for k, v in need.items():
            if seen.get(k, 0) >= v:
                continue
            seen[k] = v
            out.append((k, v))
        return out

    def _commit(self, tok, reads, writes):
        for b in reads:
            b.r.append(tok)
            if len(b.r) > 16:
                d = {}
                for (k, v) in b.r:
                    if d.get(k, 0) < v:
                        d[k] = v
                b.r = list(d.items())
        for b in writes:
            b.w = [tok]
            b.r = []

    def _flush_pending(self, e):
        pend = self.pending.get(e)
        if not pend:
            return
        idx = self.pending_idx[e]
        key = "E_" + e
        self.cnt[key] += 1
        tok = (key, self.cnt[key])
        it = self.streams[e][idx]
        self.streams[e][idx] = ("i", it[1], self.sems[key], 1)
        for (r_, w_) in pend:
            self._commit(tok, r_, w_)
        self.pending[e] = []
        self.pend_r[e] = set()
        self.pend_w[e] = set()

    def _resolve_pending(self, eng, reads, writes):
        for e in ENGS:
            if e == eng or not self.pending.get(e):
                continue
            pr, pw = self.pend_r[e], self.pend_w[e]
            hit = any(id(b) in pw for b in reads) or any((id(b) in pr or id(b) in pw) for b in writes)
            if hit:
                self._flush_pending(e)

    def op(self, eng, fn, reads=(), writes=(), signal=True):
        self._resolve_pending(eng, reads, writes)
        waits = self._waits(eng, reads, writes)
        st = self.streams[eng]
        for (k, v) in waits:
            st.append(("w", self.sems[k], v))
        if signal:
            key = "E_" + eng
            self.cnt[key] += 1
            tok = (key, self.cnt[key])
            st.append(("i", fn, self.sems[key], 1))
            pend = self.pending.get(eng)
            if pend:
                for (r_, w_) in pend:
                    self._commit(tok, r_, w_)
                self.pending[eng] = []
                self.pend_r[eng] = set()
                self.pend_w[eng] = set()
            self._commit(tok, reads, writes)
            return tok
        st.append(("i", fn, None, 0))
        self.pending.setdefault(eng, []).append((list(reads), list(writes)))
        self.pending_idx[eng] = len(st) - 1
        self.pend_r[eng].update(id(b) for b in reads)
        self.pend_w[eng].update(id(b) for b in writes)
        return None

    def dma(self, q, out_ap, in_ap, reads=(), writes=(), **kw):
        self._resolve_pending(q, reads, writes)
        waits = self._waits(q, reads, writes)
        st = self.streams[q]
        for (k, v) in waits:
            st.append(("w", self.sems[k], v))
        if q == "pool":
            semkey = "S_%d" % self.nswdma
            self.nswdma += 1
        else:
            semkey = "D_%d" % (self.ndma % self.NDSEM)
            self.ndma += 1
        self._mksem(semkey)
        if self.cnt[semkey] > 0 and self.seen[q].get(semkey, 0) < self.cnt[semkey]:
            self.seen[q][semkey] = self.cnt[semkey]
            st.append(("w", self.sems[semkey], self.cnt[semkey]))
        self.cnt[semkey] += 16
        tok = (semkey, self.cnt[semkey])
        st.append(("i", (lambda e, o=out_ap, i=in_ap, kw=kw: e.dma_start(out=o, in_=i, **kw)),
                   self.sems[semkey], 16))
        self._commit(tok, reads, writes)
        return tok

    def barrier(self, engs=ENGS):
        for e in ENGS:
            self._flush_pending(e)
        for e in engs:
            st = self.streams[e]
            for k, v in self.cnt.items():
                if v > 0 and self.seen[e].get(k, 0) < v:
                    self.seen[e][k] = v
                    st.append(("w", self.sems[k], v))

    def emit(self):
        names = {"pe": "tensor", "act": "scalar", "dve": "vector", "pool": "gpsimd", "sp": "sync"}
        with self.nc.Block() as block:
            for e in ENGS:
                stream = self.streams[e]

                def body(engobj, stream=stream):
                    for it in stream:
                        if it[0] == "w":
                            engobj.wait_ge(it[1], it[2])
                        else:
                            ins = it[1](engobj)
                            if it[2] is not None:
                                ins.then_inc(it[2], it[3])
                getattr(block, names[e])(body)

    def act(self, out, in_, func, reads, writes, **kw):
        return self.op("act", lambda e: e.activation(out=out, in_=in_, func=func, **kw), reads, writes)

    def tt(self, eng, out, in0, in1, op, reads, writes):
        return self.op(eng, lambda e: e.tensor_tensor(out=out, in0=in0, in1=in1, op=op), reads, writes)

    def ts(self, eng, out, in0, s1, s2, op0, op1, reads, writes):
        if op1 is None:
            return self.op(eng, lambda e: e.tensor_scalar(out=out, in0=in0, scalar1=s1, scalar2=None, op0=op0),
                           reads, writes)
        return self.op(eng, lambda e: e.tensor_scalar(out=out, in0=in0, scalar1=s1, scalar2=s2, op0=op0, op1=op1),
                       reads, writes)

    def stt(self, eng, out, in0, scalar, in1, op0, op1, reads, writes):
        return self.op(eng, lambda e: e.scalar_tensor_tensor(out=out, in0=in0, scalar=scalar, in1=in1,
                                                             op0=op0, op1=op1), reads, writes)

    def cp(self, eng, out, in_, reads, writes):
        if eng == "act":
            return self.op("act", lambda e: e.copy(out=out, in_=in_), reads, writes)
        return self.op(eng, lambda e: e.tensor_copy(out=out, in_=in_), reads, writes)

    def mm(self, out, lhsT, rhs, start, stop, reads=(), writes=(), signal=False, tp=None):
        if tp is None:
            fn = lambda e: e.matmul(out, lhsT=lhsT, rhs=rhs, start=start, stop=stop)
        else:
            fn = lambda e: e.matmul(out, lhsT=lhsT, rhs=rhs, start=start, stop=stop, tile_position=tp)
        return self.op("pe", fn, reads, writes, signal=signal)

    def tr(self, out, in_, ident, reads=(), writes=(), signal=False):
        return self.op("pe", lambda e: e.transpose(out=out, in_=in_, identity=ident), reads, writes, signal=signal)


class Ring:
    def __init__(self, tiles):
        self.tiles = tiles
        self.bufs = [Buf() for _ in tiles]
        self.i = 0

    def next(self):
        t, b = self.tiles[self.i], self.bufs[self.i]
        self.i = (self.i + 1) % len(self.tiles)
        return t, b


def build(S, debug=False, phases=(0, 1, 2, 3, 4)):
    assert S % 512 == 0
    NT = S // 512
    NC = S // T
    nc = bass.Bass("TRN2", target_bir_lowering=False)

    def din(name, shape, dt=F32):
        return nc.dram_tensor(name, list(shape), dt, kind="ExternalInput").ap()

    def dscr(name, shape, dt):
        return nc.dram_tensor(name, list(shape), dt, kind="Internal").ap()

    x = din("x", [S, D])
    pos = din("pos", [1, S], I32)
    cpm = din("cpm", [128, 8])
    ppd = din("pp", [128, NPP])
    bmod_row = din("bmod_row", [1, 6 * D])
    fing = din("fing", [1, D])
    w_mod = din("w_mod", [D, 6 * D])
    w_in = din("w_in", [D, 1280])
    w_uq = din("w_uq", [QL, 1024])
    w_ukn = din("w_ukn", [KVL, 512])
    w_uv = din("w_uv", [KVL, 512])
    w_glu = din("w_glu", [DSSM, 1024])
    w_out = din("w_out", [D, D])
    w_up = din("w_up", [D, 2 * DFF])
    w_down = din("w_down", [DFF, D])
    lam2 = din("lam2", [128, 3, 16])
    c2 = din("c2", [128, 2, 16, 32])
    b2 = din("b2", [128, 2, 16, 128])
    lam1 = din("lam1", [128, 3, 4, 64])
    b1 = din("b1", [128, 2, 4, 128])
    cst = din("cst", [128, 4, 128])
    out = nc.dram_tensor("out", [S, D], F32, kind="ExternalOutput").ap()

    dQN = dscr("dQN", [128, 4, S], BF16)
    dQR = dscr("dQR", [128, 4, S], BF16)
    dKN = dscr("dKN", [128, 4, S], BF16)
    dKR = dscr("dKR", [64, S], BF16)
    dV = dscr("dV", [S // 128, 128, 512], BF16)
    dAN = dscr("dAN", [128, 4, S], BF16)
    dSN = dscr("dSN", [128, 4, S], BF16)
    dWUP = dscr("dWUP", [D, 2 * DFF], BF16)
    dWIN = dscr("dWIN", [D, 1280], BF16)
    dWUQ = dscr("dWUQ", [QL, 1024], BF16)
    dWUKN = dscr("dWUKN", [KVL, 512], BF16)
    dWUV = dscr("dWUV", [KVL, 512], BF16)
    dWGLU = dscr("dWGLU", [DSSM, 1024], BF16)
    dWOUT = dscr("dWOUT", [D, D], BF16)
    dWDN = dscr("dWDN", [DFF, D], BF16)
    bWc = Buf()
    bQ = [Buf() for _ in range(NT)]
    bK = [Buf() for _ in range(NT)]
    bAN = [Buf() for _ in range(NT)]
    bSN = [Buf() for _ in range(NT)]
    bWUP = Buf()

    dbg = {}

    def dout(name, shape, dt=F32):
        t = nc.dram_tensor(name, list(shape), dt, kind="ExternalOutput").ap()
        dbg[name] = t
        return t

    with ExitStack() as top:
        kb = KB(nc, top)

        def sb(st, name, shape, dt):
            return st.enter_context(nc.sbuf_tensor(name, list(shape), dt))

        def pm(st, name, shape, dt=F32):
            return st.enter_context(nc.psum_tensor(name, list(shape), dt))

        def ring(st, name, shape, dt, n, psum=False):
            return Ring([(pm if psum else sb)(st, "%s%d" % (name, i), shape, dt) for i in range(n)])

        ppt = sb(top, "ppt", [128, NPP], F32); b_pp = Buf()
        ident_b = sb(top, "ident_b", [128, 128], BF16)
        ident_f = sb(top, "ident_f", [128, 128], F32)
        tri_b = sb(top, "tri_b", [128, 128], BF16)
        selA = sb(top, "selA", [128, 128], BF16)
        selB = sb(top, "selB", [128, 128], BF16)
        ones_b = sb(top, "ones_b", [128, 128], BF16)
        b_c = Buf()
        modf = sb(top, "modf", [128, 48], F32); b_modf = Buf()
        a1 = sb(top, "a1", [128, 8], F32)
        a2 = sb(top, "a2", [128, 8], F32)
        qgs = sb(top, "qgs", [128, 3], F32)
        gab = sb(top, "gab", [128, D], F32)
        gfb = sb(top, "gfb", [128, D], F32)
        b_gb = Buf()
        negm = sb(top, "negm", [128, 4], F32); b_negm = Buf()

        def P_(name, i=None):
            a, b = PP[name]
            if i is None:
                return ppt[:, a:b]
            return ppt[:, a + i:a + i + 1]

        kb.dma("sp", ppt[:], ppd, writes=[b_pp])
        kb.dma("pool", ident_b[:], cst[:, 0, :], writes=[b_c])
        kb.dma("sp", ident_f[:], cst[:, 0, :], writes=[b_c])
        kb.dma("pool", tri_b[:], cst[:, 1, :], writes=[b_c])
        kb.dma("pool", selA[:], cst[:, 2, :], writes=[b_c])
        kb.dma("pool", selB[:], cst[:, 3, :], writes=[b_c])
        kb.op("dve", lambda e: e.memset(ones_b[:], 1.0), writes=[b_c])

        klag = dscr("dKLAG", [128, 4, T, 128], BF16); b_klag = Buf()
        bp = dscr("dBP", [128, 4, T, 2, 128], BF16); b_bp = Buf()
        cpt = dscr("dCPT", [128, 16, T + 1, 2, 32], BF16); b_cpt = Buf()
        NLD = max(1, int(math.log2(NC)))
        ld = sb(top, "ld", [128, 16, NLD + 1, 3], F32); b_ld = Buf()
        dU = dscr("dU", [128, 4, S], BF16)
        bU = [Buf() for _ in range(NT)]

        for dst_, src_ in ((dWIN, w_in), (dWUQ, w_uq), (dWUKN, w_ukn), (dWUV, w_uv), (dWGLU, w_glu)):
            kb.dma("pool", dst_, src_, writes=[bWc])
        if 4 in phases:
            kb.dma("pool", dWOUT, w_out, writes=[bWc])
            for c in range(2):
                kb.dma("pool", dWDN[c * 1408:(c + 1) * 1408, :], w_down[c * 1408:(c + 1) * 1408, :], writes=[bWc])
            for c in range(4):
                kb.dma("pool", dWUP[c * 256:(c + 1) * 256, :], w_up[c * 256:(c + 1) * 256, :], writes=[bWUP])
        with ExitStack() as st:
            def mod_part():
                cpt_ = sb(st, "cpm_t", [128, 8], F32); b_cond = Buf()
                cond = sb(st, "cond", [128, 8], F32)
                kb.dma("sp", cpt_[:], cpm, writes=[b_cond])
                kb.act(cond[:], cpt_[:], AF.Silu, [b_cond], [b_cond])
                wm = ring(st, "wm", [128, 8, 512], F32, 2)
                psm = pm(st, "psm", [128, 512]); b_psm = Buf()
                psb = pm(st, "psb", [128, 2, 512]); b_psb = Buf()
                for v in range(12):
                    wt, wb = wm.next()
                    kb.dma("sp", wt[:], w_mod[:, v * 512:(v + 1) * 512].rearrange("(kt p) n -> p kt n", p=128), writes=[wb])
                    for m in range(4):
                        for kt in range(8):
                            kb.mm(psm[:, v * 4 + m:v * 4 + m + 1], wt[:, kt, m * 128:(m + 1) * 128], cond[:, kt:kt + 1],
                                  kt == 0, kt == 7, reads=[wb, b_cond], writes=[b_psm], signal=(kt == 7))
                kb.tt("dve", modf[:], psm[:, 0:48], P_("bmod"), ALU.add, [b_psm, b_pp], [b_modf])
                kb.stt("dve", a1[:], modf[:, 8:16], 1.0, P_("ln1"), ALU.add, ALU.mult, [b_modf, b_pp], [b_modf])
                kb.stt("dve", a2[:], modf[:, 32:40], 1.0, P_("ln2"), ALU.add, ALU.mult, [b_modf, b_pp], [b_modf])
                kb.ts("dve", qgs[:], P_("qg"), SCALE, None, ALU.mult, None, [b_pp], [b_modf])
                ones_f = sb(st, "ones_f", [128, 128], F32); b_of = Buf()
                kb.op("dve", lambda e: e.memset(ones_f[:], 1.0), writes=[b_of])
                dg = ring(st, "dg", [128, 128], F32, 2)
                for vi, dst in ((2, gab), (5, gfb)):
                    for kt in range(8):
                        d_t, d_b = dg.next()
                        kb.ts("dve", d_t[:], ident_f[:], modf[:, vi * 8 + kt:vi * 8 + kt + 1], None, ALU.mult, None,
                              [b_c, b_modf], [d_b])
                        kb.mm(psb[:, kt // 4, (kt % 4) * 128:(kt % 4 + 1) * 128], ones_f[:], d_t[:], True, True,
                              reads=[b_of, d_b], writes=[b_psb], signal=True)
                    kb.cp("dve", dst[:].rearrange("p (n f) -> p n f", n=2), psb[:], [b_psb], [b_gb])
                if debug:
                    kb.dma("sp", dout("d_modf", [128, 48]), modf[:], reads=[b_modf])
                    kb.dma("sp", dout("d_gab", [128, D]), gab[:], reads=[b_gb])


            if 3 in phases:
                ssm_tables(kb, nc, st, sb, pm, lam2, c2, b2, lam1, b1, klag, bp, cpt, ld, NLD, ident_f, ppt,
                           b_klag, b_bp, b_cpt, b_ld, b_pp, b_c, debug, dout, mid=mod_part)
            else:
                mod_part()
        kb.barrier()

        if 1 in phases:
            with ExitStack() as st:
                phase1(kb, nc, st, sb, pm, ring, locals())
            kb.barrier()

        if debug and 1 in phases and "dbg1" not in SKIP:
            with ExitStack() as st:
                kb.dma("sp", dout("d_U", [128, 4, S], BF16), dU, reads=bU)
                kb.dma("sp", dout("d_negm", [128, 4]), negm[:], reads=[b_negm])
                for nm, src, shp in (("d_QN", dQN, [128, 4, S]), ("d_QR", dQR, [128, 4, S]), ("d_KN", dKN, [128, 4, S]),
                                     ("d_KR", dKR, [64, S]), ("d_V", dV, [S // 128, 128, 512])):
                    kb.dma("sp", dout(nm, shp, BF16), src, reads=bQ + bK)
            kb.barrier()

        if 2 in phases:
            with ExitStack() as st:
                phase2(kb, nc, st, sb, pm, ring, locals())
            kb.barrier()
            if debug:
                kb.dma("sp", dout("d_AN", [128, 4, S], BF16), dAN, reads=bAN)
                kb.barrier()
        if 3 in phases:
            with ExitStack() as st:
                try:
                    phase3(kb, nc, st, sb, pm, ring, locals())
                except _Stop:
                    pass
            kb.barrier()
            if debug:
                kb.dma("sp", dout("d_SN", [128, 4, S], BF16), dSN, reads=bSN)
                kb.barrier()
        if 4 in phases:
            with ExitStack() as st:
                phase4(kb, nc, st, sb, pm, ring, locals())
            kb.barrier()

        kb.barrier(("sp",))
        kb.emit()
    return nc, dbg


def phase1(kb, nc, st, sb, pm, ring, E):
    g = E
    S, NT, NC = g["S"], g["NT"], g["NC"]
    x, pos = g["x"], g["pos"]
    ppt, P_ = g["ppt"], g["P_"]
    b_pp, b_c, b_modf = g["b_pp"], g["b_c"], g["b_modf"]
    ident_b, ones_b, selA, selB = g["ident_b"], g["ones_b"], g["selA"], g["selB"]
    modf, a1, qgs = g["modf"], g["a1"], g["qgs"]
    dU, bU = g["dU"], g["bU"]
    negm, b_negm = g["negm"], g["b_negm"]
    debug, dout = g["debug"], g["dout"]

    if "noalloc1" in SKIP:
        return
    if "bigalloc" in SKIP:
        import os
        n = int(os.environ.get("BIGKB", "100"))
        big = sb(st, "big", [128, n * 256], F32)
        print("sbuf remaining", nc.sbuf_bytes_remaining)
        return
    WIN = sb(st, "WIN", [128, 8, 1280], BF16)
    WUQ = sb(st, "WUQ", [128, 3, 1024], BF16)
    WUKN = sb(st, "WUKN", [128, 2, 512], BF16)
    WUV = sb(st, "WUV", [128, 2, 512], BF16)
    b_w = Buf()
    bWc = g["bWc"]
    kb.dma("sp", WIN[:], g["dWIN"].rearrange("(kt p) n -> p kt n", p=128), reads=[bWc], writes=[b_w])
    kb.dma("sp", WUQ[:], g["dWUQ"].rearrange("(kt p) n -> p kt n", p=128), reads=[bWc], writes=[b_w])
    kb.dma("sp", WUKN[:], g["dWUKN"].rearrange("(kt p) n -> p kt n", p=128), reads=[bWc], writes=[b_w])
    kb.dma("sp", WUV[:], g["dWUV"].rearrange("(kt p) n -> p kt n", p=128), reads=[bWc], writes=[b_w])

    if "alloc_a" in SKIP:
        return
    xs = ring(st, "xs", [128, D], F32, 3)
    posi = ring(st, "posi", [128, 512], I32, 2)
    junk = sb(st, "junk", [128, D], BF16); b_junk = Buf()
    ssq = ring(st, "ssq", [128, 8], F32, 2)
    xn = ring(st, "xn", [128, 4, D], BF16, 1)
    h1 = ring(st, "h1", [128, 8, 512], BF16, 2)
    zf = ring(st, "zf", [128, 512], F32, 6)
    sq = ring(st, "sq", [128, 512], BF16, 6)
    rs = ring(st, "rs", [128, 512], F32, 2)
    krsq = ring(st, "krsq", [64, 512], BF16, 1)
    zqn = ring(st, "zqn", [128, 3, 512], BF16, 2)
    zkvn = ring(st, "zkvn", [128, 2, 512], BF16, 2)
    rt = ring(st, "rt", [128, 512], F32, 6)
    c2t = ring(st, "c2t", [128, 512], F32, 2)
    s2t = ring(st, "s2t", [128, 512], F32, 2)
    QN = ring(st, "QN", [128, 4, 512], BF16, 2)
    QR = ring(st, "QR", [128, 4, 512], BF16, 2)
    for (qz_t, qz_b) in zip(QR.tiles, QR.bufs):
        kb.op("pool", lambda e, t=qz_t: e.memset(t[:], 0.0), writes=[qz_b])
    KN = ring(st, "KN", [128, 4, 512], BF16, 2)
    KR = ring(st, "KR", [64, 512], BF16, 2)
    VT = ring(st, "VT", [128, 4, 512], BF16, 2)
    UT = ring(st, "UT", [128, 4, 512], BF16, 2)
    qmax = sb(st, "qmax", [128, 4, NT], F32); b_qmax = Buf()
    kmax = sb(st, "kmax", [128, 4, NT], F32); b_kmax = Buf()

    if "alloc_b" in SKIP:
        return
    ptr = ring(st, "ptr", [128, 1024], BF16, 2, psum=True)
    pz = ring(st, "pz", [128, 512], F32, 4, psum=True)
    pn = ring(st, "pn", [128, 512], F32, 2, psum=True)

    if "alloc_c" in SKIP:
        return
    dQN, dQR, dKN, dKR, dV = g["dQN"], g["dQR"], g["dKN"], g["dKR"], g["dV"]
    bQ, bK = g["bQ"], g["bK"]

    def rstd_from_psum(pn_t, pn_b, n, eng="dve"):
        r_t, r_b = rs.next()
        kb.act(r_t[:], pn_t[:], AF.Sqrt, [pn_b, b_pp], [r_b], scale=1.0 / n, bias=P_("eps"))
        kb.op("dve", lambda e: e.reciprocal(out=r_t[:], in_=r_t[:]), [r_b], [r_b])
        return r_t, r_b

    fronts = {}

    fa = {}

    def front_a(tt):
        t0 = tt * 512
        pi_t, pi_b = posi.next()
        kb.dma("sp", pi_t[:], pos[:, t0:t0 + 512].partition_broadcast(128), writes=[pi_b])
        sq_t, sq_b = ssq.next()
        kb.op("dve", lambda e, t=sq_t: e.memset(t[:], 0.0), writes=[sq_b])
        xn_t, xn_b = xn.next()
        for s_ in range(4):
            x_t, x_b = xs.next()
            kb.dma("sp", x_t[:], x[t0 + s_ * 128:t0 + (s_ + 1) * 128, :], writes=[x_b])
            kb.act(junk[:], x_t[:], AF.Square, [x_b], [b_junk, sq_b], accum_out=sq_t[:, s_:s_ + 1])
            kb.act(sq_t[:, 4 + s_:5 + s_], sq_t[:, s_:s_ + 1], AF.Sqrt, [sq_b, b_pp], [sq_b], scale=1.0 / D, bias=P_("eps"))
            kb.op("dve", lambda e, t=sq_t, s_=s_: e.reciprocal(out=t[:, 4 + s_:5 + s_], in_=t[:, 4 + s_:5 + s_]), [sq_b], [sq_b])
            if s_ % 2 == 0:
                kb.ts("dve", xn_t[:, s_, :], x_t[:], sq_t[:, 4 + s_:5 + s_], None, ALU.mult, None, [x_b, sq_b], [xn_b])
            else:
                kb.act(xn_t[:, s_, :], x_t[:], AF.Copy, [x_b, sq_b], [xn_b], scale=sq_t[:, 4 + s_:5 + s_])
        a_t, a_b = rt.next()
        k_t, k_b = rt.next()
        kb.cp("dve", a_t[:], pi_t[:], [pi_b], [a_b])
        kb.ts("dve", a_t[:], a_t[:], P_("invf"), None, ALU.mult, None, [a_b, b_pp], [a_b])
        kb.ts("dve", k_t[:], a_t[:], 1.0 / (2 * math.pi), MAGIC, ALU.mult, ALU.add, [a_b], [k_b])
        kb.ts("dve", k_t[:], k_t[:], -MAGIC, None, ALU.add, None, [k_b], [k_b])
        kb.stt("dve", a_t[:], k_t[:], -C1, a_t[:], ALU.mult, ALU.add, [k_b, a_b], [a_b])
        kb.stt("dve", a_t[:], k_t[:], -C2, a_t[:], ALU.mult, ALU.add, [k_b, a_b], [a_b])
        kb.stt("dve", a_t[:], k_t[:], -C3, a_t[:], ALU.mult, ALU.add, [k_b, a_b], [a_b])
        kb.ts("dve", a_t[:], a_t[:], 3.1415925, -3.1415925, ALU.min, ALU.max, [a_b], [a_b])
        kb.stt("dve", k_t[:], a_t[:], -1.0, a_t[:], ALU.mult, ALU.max, [a_b], [k_b])
        c_t, c_b = c2t.next()
        s_t, s_b = s2t.next()
        kb.act(c_t[:], k_t[:], AF.Sin, [k_b, b_pp], [c_b], scale=-1.0, bias=P_("halfpi"))
        kb.act(s_t[:], a_t[:], AF.Sin, [a_b, b_pp], [s_b], scale=P_("sgn"))
        fa[tt] = (xn_t, xn_b, c_t, c_b, s_t, s_b)

    def front_b(tt):
        xn_t, xn_b, c_t, c_b, s_t, s_b = fa.pop(tt)
        h_t, h_b = h1.next()
        for kt in range(8):
            p_t, p_b = ptr.next()
            for s_ in range(4):
                kb.tr(p_t[:, s_ * 128:(s_ + 1) * 128], xn_t[:, s_, kt * 128:(kt + 1) * 128], ident_b[:],
                      reads=[xn_b, b_c], writes=[p_b], signal=(s_ == 3))
            kb.act(h_t[:, kt, :], p_t[:, 0:512], AF.Identity, [p_b, b_modf], [h_b],
                   scale=a1[:, kt:kt + 1], bias=modf[:, kt:kt + 1])
        if debug and tt == 0:
            tmp = sb(st, "dbgh1", [128, 8, 512], F32); bt = Buf()
            kb.cp("dve", tmp[:], h_t[:], [h_b], [bt])
            kb.dma("sp", dout("d_h1", [128, 8, 512]), tmp[:], reads=[bt])
        fronts[tt] = (h_t, h_b, c_t, c_b, s_t, s_b)

    def tile_body(tt):
        t0 = tt * 512
        h_t, h_b, c_t, c_b, s_t, s_b = fronts.pop(tt)

        def inproj(col0, M):
            z_t, z_b = pz.next()
            for kt in range(8):
                kb.mm(z_t[0:M, :], WIN[:, kt, col0:col0 + M], h_t[:, kt, :], kt == 0, kt == 7,
                      reads=[b_w, h_b], writes=[z_b], signal=(kt == 7))
            return z_t, z_b

        def normed_start(cols):
            lst = []
            for i, c0 in enumerate(cols):
                z_t, z_b = inproj(c0, 128)
                f_t, f_b = zf.next()
                kb.cp("act", f_t[:], z_t[:], [z_b], [f_b])
                q_t, q_b = sq.next()
                kb.tt("dve", q_t[:], f_t[:], f_t[:], ALU.mult, [f_b], [q_b])
                lst.append((f_t, f_b, q_t, q_b))
            return lst

        def normed_finish(lst, n, gcol, dst_t, dst_b):
            pn_t, pn_b = pn.next()
            for i, (f_t, f_b, q_t, q_b) in enumerate(lst):
                kb.mm(pn_t[:], ones_b[:], q_t[:], i == 0, i == len(lst) - 1, reads=[q_b, b_c], writes=[pn_b],
                      signal=(i == len(lst) - 1))
            r_t, r_b = rstd_from_psum(pn_t, pn_b, n)
            for i, (f_t, f_b, q_t, q_b) in enumerate(lst):
                kb.stt("dve", dst_t[:, i, :], f_t[:], gcol(i), r_t[:], ALU.mult, ALU.mult, [f_b, r_b, b_pp, b_modf], [dst_b])

        def rope(pa, pa_b, pb, pb_b, dst, dst_b, M):
            t1, t1b = rt.next()
            t2, t2b = rt.next()
            kb.tt("dve", t1[0:M, :], pa[0:M, :], c_t[0:M, :], ALU.mult, [pa_b, c_b], [t1b])
            kb.tt("dve", t2[0:M, :], pb[0:M, :], s_t[0:M, :], ALU.mult, [pb_b, s_b], [t2b])
            kb.tt("dve", dst, t1[0:M, :], t2[0:M, :], ALU.add, [t1b, t2b], [dst_b])

        if tt + 1 < NT:
            front_a(tt + 1)
        lq = normed_start([0, 128, 256])
        lk = normed_start([384, 512])
        za, za_b = inproj(640, 64)
        zb, zb_b = inproj(704, 64)
        kr_t, kr_b = KR.next()
        rope(za, za_b, zb, zb_b, kr_t[:, :], kr_b, 64)
        u_t, u_b = UT.next()
        for ft in range(4):
            z_t, z_b = inproj(768 + ft * 128, 128)
            kb.cp("act", u_t[:, ft, :], z_t[:], [z_b], [u_b])
        kb.dma("sp", dU[:, :, t0:t0 + 512], u_t[:], reads=[u_b], writes=[bU[tt]])
        zq_t, zq_b = zqn.next()
        normed_finish(lq, QL, lambda i: qgs[:, i:i + 1], zq_t, zq_b)
        zk_t, zk_b = zkvn.next()
        normed_finish(lk, KVL, lambda i: P_("kvg", i), zk_t, zk_b)
        if tt + 1 < NT:
            front_b(tt + 1)
        qn_t, qn_b = QN.next()
        qr_t, qr_b = QR.next()

        def qproj(mt):
            z_t, z_b = pz.next()
            for kt in range(3):
                kb.mm(z_t[:], WUQ[:, kt, mt * 128:(mt + 1) * 128], zq_t[:, kt, :], kt == 0, kt == 2,
                      reads=[b_w, zq_b], writes=[z_b], signal=(kt == 2))
            return z_t, z_b

        for h in range(4):
            z_t, z_b = qproj(h)
            kb.cp("act", qn_t[:, h, :], z_t[:], [z_b], [qn_b])
        for pr in range(2):
            pa, pa_b = qproj(4 + pr)
            pb, pb_b = qproj(6 + pr)
            t1, t1b = rt.next()
            t2, t2b = rt.next()
            kb.tt("dve", t1[:], pa[:], c_t[:], ALU.mult, [pa_b, c_b], [t1b])
            kb.tt("dve", t2[:], pb[:], s_t[:], ALU.mult, [pb_b, s_b], [t2b])
            kb.tt("dve", qr_t[0:64, 2 * pr, :], t1[0:64, :], t2[0:64, :], ALU.add, [t1b, t2b], [qr_b])
            kb.tt("dve", qr_t[64:128, 2 * pr + 1, :], t1[64:128, :], t2[64:128, :], ALU.add, [t1b, t2b], [qr_b])
        kn_t, kn_b = KN.next()
        for h in range(4):
            z_t, z_b = pz.next()
            for kt in range(2):
                kb.mm(z_t[:], WUKN[:, kt, h * 128:(h + 1) * 128], zk_t[:, kt, :], kt == 0, kt == 1,
                      reads=[b_w, zk_b], writes=[z_b], signal=(kt == 1))
            kb.cp("act", kn_t[:, h, :], z_t[:], [z_b], [kn_b])
        v_t, v_b = VT.next()
        for s_ in range(4):
            z_t, z_b = pz.next()
            for kt in range(2):
                kb.mm(z_t[:], zk_t[:, kt, s_ * 128:(s_ + 1) * 128], WUV[:, kt, :], kt == 0, kt == 1,
                      reads=[b_w, zk_b], writes=[z_b], signal=(kt == 1))
            kb.cp("act" if s_ % 2 else "dve", v_t[:, s_, :], z_t[:], [z_b], [v_b])
        for h in range(4):
            p_t, p_b = pn.next()
            q1, q1b = sq.next()
            kb.act(q1[:], qn_t[:, h, :], AF.Square, [qn_b], [q1b])
            q2, q2b = sq.next()
            kb.act(q2[:], qr_t[:, h, :], AF.Square, [qr_b], [q2b])
            kb.mm(p_t[:], ones_b[:], q1[:], True, False, reads=[q1b, b_c], writes=[p_b])
            kb.mm(p_t[:], ones_b[:], q2[:], False, True, reads=[q2b, b_c], writes=[p_b], signal=True)
            kb.op("dve", lambda e, h=h, p_t=p_t, tt=tt: e.reduce_max(out=qmax[:, h, tt:tt + 1], in_=p_t[:], axis=AX.X), [p_b], [b_qmax])
        k2, k2b = krsq.next()
        kb.act(k2[0:64, :], kr_t[:, :], AF.Square, [kr_b], [k2b])
        for h in range(4):
            p_t, p_b = pn.next()
            k1, k1b = sq.next()
            kb.act(k1[:], kn_t[:, h, :], AF.Square, [kn_b], [k1b])
            kb.mm(p_t[:], ones_b[:], k1[:], True, False, reads=[k1b, b_c], writes=[p_b])
            kb.mm(p_t[:], ones_b[0:64, :], k2[0:64, :], False, True, reads=[k2b, b_c], writes=[p_b], signal=True)
            kb.op("dve", lambda e, h=h, p_t=p_t, tt=tt: e.reduce_max(out=kmax[:, h, tt:tt + 1], in_=p_t[:], axis=AX.X), [p_b], [b_kmax])
        kb.dma("sp", dQN[:, :, t0:t0 + 512], qn_t[:], reads=[qn_b], writes=[bQ[tt]])
        kb.dma("sp", dQR[:, :, t0:t0 + 512], qr_t[:], reads=[qr_b], writes=[bQ[tt]])
        kb.dma("sp", dKN[:, :, t0:t0 + 512], kn_t[:], reads=[kn_b], writes=[bK[tt]])
        kb.dma("sp", dKR[:, t0:t0 + 512], kr_t[:], reads=[kr_b], writes=[bK[tt]])
        kb.dma("sp", dV[tt * 4:(tt + 1) * 4].rearrange("s p d -> p s d"), v_t[:], reads=[v_b], writes=[bK[tt]])

    front_a(0)
    front_b(0)
    for tt in range(NT):
        tile_body(tt)
    mx = sb(st, "mx", [128, 12], F32); b_mx = Buf()
    kb.op("dve", lambda e: e.reduce_max(out=mx[:, 0:4], in_=qmax[:], axis=AX.X), [b_qmax], [b_mx])
    kb.op("dve", lambda e: e.reduce_max(out=mx[:, 4:8], in_=kmax[:], axis=AX.X), [b_kmax], [b_mx])
    kb.tt("dve", mx[:, 8:12], mx[:, 0:4], mx[:, 4:8], ALU.mult, [b_mx], [b_mx])
    kb.act(mx[:, 8:12], mx[:, 8:12], AF.Sqrt, [b_mx], [b_mx])
    kb.ts("dve", negm[:], mx[:, 8:12], -1.01, None, ALU.mult, None, [b_mx], [b_negm])


def phase2(kb, nc, st, sb, pm, ring, E):
    g = E
    S, NT = g["S"], g["NT"]
    ppt, P_ = g["ppt"], g["P_"]
    b_pp, b_c = g["b_pp"], g["b_c"]
    ident_b, ones_b, tri_b = g["ident_b"], g["ones_b"], g["tri_b"]
    negm, b_negm = g["negm"], g["b_negm"]
    dQN, dQR, dKN, dKR, dV, dAN = g["dQN"], g["dQR"], g["dKN"], g["dKR"], g["dV"], g["dAN"]
    bQ, bK, bAN = g["bQ"], g["bK"], g["bAN"]

    KNs = sb(st, "KNs", [128, 4, S], BF16)
    KRs = sb(st, "KRs", [128, S], BF16)
    Vs = sb(st, "Vs", [128, S // 128, 512], BF16)
    bKV = [Buf() for _ in range(NT)]
    kv_done = set()

    def load_kv(tt):
        if tt >= NT or tt in kv_done:
            return
        kv_done.add(tt)
        t0 = tt * 512
        kb.dma("sp", KNs[:, :, t0:t0 + 512], dKN[:, :, t0:t0 + 512], reads=[bK[tt]], writes=[bKV[tt]])
        kb.dma("sp", KRs[0:64, t0:t0 + 512], dKR[:, t0:t0 + 512], reads=[bK[tt]], writes=[bKV[tt]])
        kb.dma("sp", KRs[64:128, t0:t0 + 512], dKR[:, t0:t0 + 512], reads=[bK[tt]], writes=[bKV[tt]])
        kb.dma("sp", Vs[:, tt * 4:(tt + 1) * 4, :], dV[tt * 4:(tt + 1) * 4].rearrange("s p d -> p s d"),
               reads=[bK[tt]], writes=[bKV[tt]])

    Qn = ring(st, "Qn", [128, 4, 512], BF16, 2)
    Qr = ring(st, "Qr", [128, 4, 512], BF16, 2)
    PT = ring(st, "PT", [128, 512], BF16, 4)
    AFt = ring(st, "AFt", [128, 4, 512], F32, 2)
    rl = ring(st, "rl", [128, 512], F32, 1)
    lacc = ring(st, "lacc", [128, 512], F32, 2)
    lhi = ring(st, "lhi", [128, 512], BF16, 2)
    sqa = ring(st, "sqa", [128, 512], BF16, 1)
    rsa = ring(st, "rsa", [128, 512], F32, 1)
    AN = ring(st, "AN", [128, 4, 512], BF16, 1)
    psS = ring(st, "psS", [128, 512], F32, 3, psum=True)
    psO = ring(st, "psO", [128, 512], F32, 2, psum=True)
    psL = ring(st, "psL", [128, 512], F32, 2, psum=True)
    psN = ring(st, "psN", [128, 512], F32, 1, psum=True)

    deferred = []

    def flush():
        while deferred:
            deferred.pop(0)()

    for j in range(NT):
        t0 = j * 512
        qn_t, qn_b = Qn.next()
        qr_t, qr_b = Qr.next()
        load_kv(j)
        kb.dma("sp", qn_t[:], dQN[:, :, t0:t0 + 512], reads=[bQ[j]], writes=[qn_b])
        kb.dma("sp", qr_t[:], dQR[:, :, t0:t0 + 512], reads=[bQ[j]], writes=[qr_b])
        load_kv(j + 1)
        load_kv(j + 2)
        af_t, af_b = AFt.next()
        nk = 4 * (j + 1)
        for h in range(4):
            o_t, o_b = psO.next()
            l_t, l_b = psL.next()
            r0 = (h % 2) * 64
            pend = None

            la_t, la_b = lacc.next()
            lst = {"init": False}
            any_dve = (j >= 1)

            def stageC(kt, c0, p_t, p_b):
                i_ = kt - 4 * j
                use_dve = (kt % 2 == 1) and i_ < 0
                kb.mm(o_t[:, c0:512], Vs[:, kt, h * 128:(h + 1) * 128], p_t[:, c0:512], kt == 0, kt == nk - 1,
                      reads=[p_b, bKV[kt // 4]], writes=[o_b], signal=use_dve)
                if use_dve:
                    if not lst["init"]:
                        kb.cp("dve", la_t[:], p_t[:], [p_b], [la_b])
                        lst["init"] = True
                    else:
                        kb.tt("dve", la_t[:], la_t[:], p_t[:], ALU.add, [p_b, la_b], [la_b])
                else:
                    kb.mm(l_t[:, c0:512], ones_b[:], p_t[:, c0:512], kt == 0, (kt == nk - 1) and not any_dve,
                          reads=[p_b, b_c], writes=[o_b, l_b], signal=True)

            pend = []
            for kt in range(nk):
                i = kt - 4 * j
                c0 = 128 * i if i > 0 else 0
                s_t, s_b = psS.next()
                kb.mm(s_t[:, c0:512], KNs[:, h, kt * 128:(kt + 1) * 128], qn_t[:, h, c0:512], True, False,
                      reads=[bKV[kt // 4], qn_b], writes=[s_b], signal=False)
                kb.mm(s_t[:, c0:512], KRs[:, kt * 128:(kt + 1) * 128], qr_t[:, h, c0:512],
                      False, i < 0, reads=[bKV[kt // 4], qr_b], writes=[s_b], signal=(i < 0))
                if i >= 0:
                    kb.mm(s_t[:, c0:c0 + 128], ident_b[:], tri_b[:], False, True, reads=[b_c], writes=[s_b], signal=True)
                p_t, p_b = PT.next()
                kb.act(p_t[:, c0:512], s_t[:, c0:512], AF.Exp, [s_b, b_negm], [p_b], bias=negm[:, h:h + 1], scale=1.0)
                pend.append((kt, c0, p_t, p_b))
                if len(pend) > 2:
                    stageC(*pend.pop(0))
                if kt == 3:
                    flush()
            while pend:
                stageC(*pend.pop(0))
            def fin_head(h=h, o_t=o_t, o_b=o_b, l_t=l_t, l_b=l_b, la_t=la_t, la_b=la_b, any_dve=any_dve, af_t=af_t, af_b=af_b):
                if any_dve:
                    hi_t, hi_b = lhi.next()
                    lo_t, lo_b = lhi.next()
                    kb.cp("dve", hi_t[:], la_t[:], [la_b], [hi_b])
                    kb.tt("dve", la_t[:], la_t[:], hi_t[:], ALU.subtract, [la_b, hi_b], [la_b])
                    kb.cp("dve", lo_t[:], la_t[:], [la_b], [lo_b])
                    kb.mm(l_t[:], ones_b[:], hi_t[:], False, False, reads=[hi_b, b_c], writes=[l_b], signal=False)
                    kb.mm(l_t[:], ones_b[:], lo_t[:], False, True, reads=[lo_b, b_c], writes=[l_b], signal=True)
                r_t, r_b = rl.next()
                kb.op("dve", lambda e, r_t=r_t, l_t=l_t: e.reciprocal(out=r_t[:], in_=l_t[:]), [l_b], [r_b])
                kb.tt("dve", af_t[:, h, :], o_t[:], r_t[:], ALU.mult, [o_b, r_b], [af_b])

            deferred.append(fin_head)
        def fin_qtile(j=j, t0=t0, af_t=af_t, af_b=af_b):
            n_t, n_b = psN.next()
            for h in range(4):
                q_t, q_b = sqa.next()
                kb.tt("dve", q_t[:], af_t[:, h, :], af_t[:, h, :], ALU.mult, [af_b], [q_b])
                kb.mm(n_t[:], ones_b[:], q_t[:], h == 0, h == 3, reads=[q_b, b_c], writes=[n_b], signal=(h == 3))
            rs_t, rs_b = rsa.next()
            kb.act(rs_t[:], n_t[:], AF.Sqrt, [n_b, b_pp], [rs_b], scale=1.0 / 512, bias=P_("eps"))
            kb.op("dve", lambda e, rs_t=rs_t: e.reciprocal(out=rs_t[:], in_=rs_t[:]), [rs_b], [rs_b])
            an_t, an_b = AN.next()
            for h in range(4):
                kb.stt("dve", an_t[:, h, :], af_t[:, h, :], P_("ag", h), rs_t[:], ALU.mult, ALU.mult,
                       [af_b, rs_b, b_pp], [an_b])
            kb.dma("sp", dAN[:, :, t0:t0 + 512], an_t[:], reads=[an_b], writes=[bAN[j]])
        deferred.append(fin_qtile)
    flush()


def phase4(kb, nc, st, sb, pm, ring, E):
    g = E
    S, NT = g["S"], g["NT"]
    x, out = g["x"], g["out"]
    ppt, P_ = g["ppt"], g["P_"]
    b_pp, b_c, b_modf, b_gb = g["b_pp"], g["b_c"], g["b_modf"], g["b_gb"]
    ident_b = g["ident_b"]
    modf, a2, gab, gfb = g["modf"], g["a2"], g["gab"], g["gfb"]
    dAN, dSN, dWUP = g["dAN"], g["dSN"], g["dWUP"]
    bAN, bSN, bWUP = g["bAN"], g["bSN"], g["bWUP"]

    WOUT = sb(st, "WOUT", [128, 8, D], BF16)
    WDN = sb(st, "WDN", [128, NKF, D], BF16)
    b_w = Buf()
    kb.dma("sp", WOUT[:], g["dWOUT"].rearrange("(kt p) n -> p kt n", p=128), reads=[g["bWc"]], writes=[b_w])
    kb.dma("sp", WDN[:], g["dWDN"].rearrange("(kt p) n -> p kt n", p=128), reads=[g["bWc"]], writes=[b_w])
    fgb = sb(st, "fgb", [128, D], F32); b_fg = Buf()
    kb.dma("sp", fgb[:], g["fing"].partition_broadcast(128), writes=[b_fg])
    wup = ring(st, "wup", [128, 8, 512], BF16, 2)
    ans = ring(st, "ans", [128, 8, 512], BF16, 1)
    xt = ring(st, "xt", [128, D], F32, 2)
    x1r = [(sb(st, "x1_%d" % i, [128, 4, D], F32), [Buf() for _ in range(4)]) for i in range(2)]
    tmp = ring(st, "tmp", [128, D], F32, 2)
    tmph = ring(st, "tmph", [128, 512], F32, 2)
    junk = sb(st, "junk4", [128, D], BF16); b_junk = Buf()
    ssq = ring(st, "ssq4", [128, 4], F32, 4)
    xn2 = ring(st, "xn2", [128, 4, D], BF16, 1)
    h2 = ring(st, "h2", [128, 8, 512], BF16, 1)
    G = ring(st, "G", [128, 514], F32, 2)
    halo = sb(st, "halo", [128, NKF, 2], F32); b_halo = [Buf() for _ in range(NKF)]
    acc = ring(st, "acc", [128, 512], F32, 2)
    gl = ring(st, "gl", [128, 512], F32, 2)
    actb = sb(st, "actb", [128, NKF, 512], BF16); b_actb = Buf()
    ptr = ring(st, "ptr4", [128, 1024], BF16, 2, psum=True)
    pz = ring(st, "pz4", [128, 512], F32, 4, psum=True)
    psX = [pm(st, "psX%d" % i, [128, 512], F32) for i in range(2)]
    b_psX = [Buf(), Buf()]
    kb.op("pool", lambda e: e.memset(halo[:], 0.0), writes=b_halo)

    def rstd_col(src_ap, reads):
        q_t, q_b = ssq.next()
        kb.op("dve", lambda e: e.memset(q_t[:], 0.0), writes=[q_b])
        kb.act(junk[:], src_ap, AF.Square, reads, [b_junk, q_b], accum_out=q_t[:, 0:1])
        kb.act(q_t[:, 1:2], q_t[:, 0:1], AF.Sqrt, [q_b, b_pp], [q_b], scale=1.0 / D, bias=P_("eps"))
        kb.op("dve", lambda e: e.reciprocal(out=q_t[:, 2:3], in_=q_t[:, 1:2]), [q_b], [q_b])
        return q_t, q_b

    gab2 = gab[:].rearrange("p (n f) -> p n f", n=2)
    gfb2 = gfb[:].rearrange("p (n f) -> p n f", n=2)
    state = {}

    def H1(tt):
        t0 = tt * 512
        a_t, a_b = ans.next()
        kb.dma("sp", a_t[:, 0:4, :], dAN[:, :, t0:t0 + 512], reads=[bAN[tt]], writes=[a_b])
        kb.dma("sp", a_t[:, 4:8, :], dSN[:, :, t0:t0 + 512], reads=[bSN[tt]], writes=[a_b])
        x1_t, x1_bs = x1r[tt % 2]
        xn_t, xn_b = xn2.next()
        for s_ in range(4):
            x_t, x_b = xt.next()
            kb.dma("sp", x_t[:], x[t0 + s_ * 128:t0 + (s_ + 1) * 128, :], writes=[x_b])
            for n in range(2):
                for kt in range(8):
                    kb.mm(psX[n][:], a_t[:, kt, s_ * 128:(s_ + 1) * 128], WOUT[:, kt, n * 512:(n + 1) * 512],
                          kt == 0, kt == 7, reads=[a_b, b_w], writes=[b_psX[n]], signal=(kt == 7))
                m_t, m_b = tmph.next()
                kb.tt("dve", m_t[:], psX[n][:], gab2[:, n, :], ALU.mult, [b_psX[n], b_gb], [m_b])
                kb.tt("dve", x1_t[:, s_, n * 512:(n + 1) * 512], m_t[:], x_t[:, n * 512:(n + 1) * 512], ALU.add,
                      [m_b, x_b], [x1_bs[s_]])
            q_t, q_b = rstd_col(x1_t[:, s_, :], [x1_bs[s_]])
            kb.act(xn_t[:, s_, :], x1_t[:, s_, :], AF.Copy, [x1_bs[s_], q_b], [xn_b], scale=q_t[:, 2:3])
        state[tt] = dict(x1=(x1_t, x1_bs), xn=(xn_t, xn_b))

    def H2(tt):
        xn_t, xn_b = state[tt]["xn"]
        h_t, h_b = h2.next()
        for kt in range(8):
            p_t, p_b = ptr.next()
            for s_ in range(4):
                kb.tr(p_t[:, s_ * 128:(s_ + 1) * 128], xn_t[:, s_, kt * 128:(kt + 1) * 128], ident_b[:],
                      reads=[xn_b, b_c], writes=[p_b], signal=(s_ == 3))
            kb.act(h_t[:, kt, :], p_t[:, 0:512], AF.Identity, [p_b, b_modf], [h_b],
                   scale=a2[:, kt:kt + 1], bias=modf[:, 24 + kt:25 + kt])
        state[tt]["h2"] = (h_t, h_b)

    def M(tt):
        h_t, h_b = state[tt]["h2"]
        for c in range(NKF // 2):
            w_t, w_b = wup.next()
            kb.dma("sp", w_t[:], dWUP[:, c * 512:(c + 1) * 512].rearrange("(kt p) n -> p kt n", p=128),
                   reads=[bWUP], writes=[w_b])
            for q in range(2):
                kt = 2 * c + q
                pg, pg_b = pz.next()
                pv, pv_b = pz.next()
                for k8 in range(8):
                    kb.mm(pg[:], w_t[:, k8, q * 256:q * 256 + 128], h_t[:, k8, :], k8 == 0, k8 == 7,
                          reads=[w_b, h_b], writes=[pg_b], signal=(k8 == 7))
                for k8 in range(8):
                    kb.mm(pv[:], w_t[:, k8, q * 256 + 128:q * 256 + 256], h_t[:, k8, :], k8 == 0, k8 == 7,
                          reads=[w_b, h_b], writes=[pv_b], signal=(k8 == 7))
                g_t, g_b = G.next()
                kb.cp("act", g_t[:, 0:2], halo[:, kt, :], [b_halo[kt]], [g_b])
                kb.cp("act", g_t[:, 2:514], pg[:], [pg_b], [g_b])
                kb.cp("act", halo[:, kt, :], g_t[:, 512:514], [g_b], [b_halo[kt]])
                ac, ac_b = acc.next()
                cw = PP["convw"][0] + kt * 3
                kb.ts("dve", ac[:], g_t[:, 2:514], ppt[:, cw + 2:cw + 3], P_("convb", kt), ALU.mult, ALU.add,
                      [g_b, b_pp], [ac_b])
                kb.stt("dve", ac[:], g_t[:, 1:513], ppt[:, cw + 1:cw + 2], ac[:], ALU.mult, ALU.add, [g_b, ac_b, b_pp], [ac_b])
                kb.stt("dve", ac[:], g_t[:, 0:512], ppt[:, cw:cw + 1], ac[:], ALU.mult, ALU.add, [g_b, ac_b, b_pp], [ac_b])
                l_t, l_b = gl.next()
                kb.act(l_t[:], ac[:], AF.Gelu_apprx_tanh, [ac_b], [l_b])
                kb.tt("dve", actb[:, kt, :], l_t[:], pv[:], ALU.mult, [l_b, pv_b], [b_actb])

    def T_(tt):
        t0 = tt * 512
        x1_t, x1_bs = state[tt]["x1"]
        for s_ in range(4):
            m_t, m_b = tmp.next()
            for n in range(2):
                for kt in range(NKF):
                    kb.mm(psX[n][:], actb[:, kt, s_ * 128:(s_ + 1) * 128], WDN[:, kt, n * 512:(n + 1) * 512],
                          kt == 0, kt == NKF - 1, reads=[b_actb, b_w], writes=[b_psX[n]], signal=(kt == NKF - 1))
                kb.tt("dve", m_t[:, n * 512:(n + 1) * 512], psX[n][:], gfb2[:, n, :], ALU.mult, [b_psX[n], b_gb], [m_b])
            kb.tt("dve", m_t[:], m_t[:], x1_t[:, s_, :], ALU.add, [m_b, x1_bs[s_]], [m_b])
            q_t, q_b = rstd_col(m_t[:], [m_b])
            o_t, o_b = tmp.next()
            kb.stt("dve", o_t[:], m_t[:], q_t[:, 2:3], fgb[:], ALU.mult, ALU.mult, [m_b, q_b, b_fg], [o_b])
            kb.dma("sp", out[t0 + s_ * 128:t0 + (s_ + 1) * 128, :], o_t[:], reads=[o_b])
        del state[tt]

    H1(0)
    H2(0)
    for tt in range(NT):
        M(tt)
        if tt + 1 < NT:
            H1(tt + 1)
        T_(tt)
        if tt + 1 < NT:
            H2(tt + 1)


def ssm_tables(kb, nc, st, sb, pm, lam2, c2, b2, lam1, b1, klag, bp, cpt, ld, NLD, ident_f, ppt,
               b_klag, b_bp, b_cpt, b_ld, b_pp, b_c, debug, dout, mid=None):
    bs = Buf()
    E = "dve"

    def tt_(o, a, b, op):
        kb.tt(E, o, a, b, op, [bs], [bs])

    def ts_(o, a, s1, s2, op0, op1=None):
        kb.ts(E, o, a, s1, s2, op0, op1, [bs], [bs])

    def stt_(o, a, sc, b, op0, op1):
        kb.stt(E, o, a, sc, b, op0, op1, [bs], [bs])

    def sincos(ang, sn, cs, t0, t1, t2):
        ts_(t0, ang, 1.0 / (2 * math.pi), MAGIC, ALU.mult, ALU.add)
        ts_(t0, t0, -MAGIC, None, ALU.add)
        stt_(t1, t0, -C1, ang, ALU.mult, ALU.add)
        stt_(t1, t0, -C2, t1, ALU.mult, ALU.add)
        stt_(t1, t0, -C3, t1, ALU.mult, ALU.add)
        ts_(t1, t1, 0.5, None, ALU.mult)
        tt_(t2, t1, t1, ALU.mult)
        ts_(t0, t2, SINC[5], None, ALU.mult)
        for k in (4, 3, 2, 1, 0):
            stt_(t0, t0, SINC[k], t2, ALU.add, ALU.mult)
        stt_(sn, t0, 1.0, t1, ALU.add, ALU.mult)
        ts_(t0, t2, COSC[5], None, ALU.mult)
        for k in (4, 3, 2, 1, 0):
            stt_(t0, t0, COSC[k], t2, ALU.add, ALU.mult)
        ts_(cs, t0, 1.0, None, ALU.add)
        tt_(t0, sn, sn, ALU.mult)
        stt_(sn, sn, 2.0, cs, ALU.mult, ALU.mult)
        ts_(cs, t0, -2.0, 1.0, ALU.mult, ALU.add)

    def lam_calc(lre, lim, ldt, F, pfx):
        n = lambda k: sb(st, pfx + k, [128, F], F32)
        dt, lr, mag, ang, sn, cs = n("dt"), n("lr"), n("mag"), n("ang"), n("sn"), n("cs")
        t0, t1, t2 = n("t0"), n("t1"), n("t2")
        abre, abim, zre, zim = n("abre"), n("abim"), n("zre"), n("zim")
        kb.act(dt[:], ldt, AF.Exp, [bs], [bs])
        ts_(lr[:], lre, -1e-4, None, ALU.min)
        tt_(t0[:], lr[:], dt[:], ALU.mult)
        kb.act(mag[:], t0[:], AF.Exp, [bs], [bs])
        tt_(ang[:], lim, dt[:], ALU.mult)
        sincos(ang[:], sn[:], cs[:], t0[:], t1[:], t2[:])
        tt_(abre[:], mag[:], cs[:], ALU.mult)
        tt_(abim[:], mag[:], sn[:], ALU.mult)
        tt_(t0[:], lr[:], lr[:], ALU.mult)
        tt_(t1[:], lim, lim, ALU.mult)
        tt_(t0[:], t0[:], t1[:], ALU.add)
        kb.op(E, lambda e: e.reciprocal(out=t0[:], in_=t0[:]), [bs], [bs])
        ts_(t1[:], abre[:], -1.0, None, ALU.add)
        tt_(t2[:], t1[:], lr[:], ALU.mult)
        tt_(zre[:], abim[:], lim, ALU.mult)
        tt_(zre[:], zre[:], t2[:], ALU.add)
        tt_(zre[:], zre[:], t0[:], ALU.mult)
        tt_(t2[:], abim[:], lr[:], ALU.mult)
        tt_(zim[:], t1[:], lim, ALU.mult)
        tt_(zim[:], t2[:], zim[:], ALU.subtract)
        tt_(zim[:], zim[:], t0[:], ALU.mult)
        return abre, abim, zre, zim, (t0, t1, t2, sn, cs)

    def powers(abre, abim, F, npow, pfx, tmps):
        pre = sb(st, pfx + "pre", [128, npow, F], F32)
        pim = sb(st, pfx + "pim", [128, npow, F], F32)
        t0, t1 = tmps[0], tmps[1]
        kb.op(E, lambda e: e.memset(pre[:, 0, :], 1.0), [bs], [bs])
        kb.op(E, lambda e: e.memset(pim[:, 0, :], 0.0), [bs], [bs])
        for k in range(npow - 1):
            tt_(t0[:], pre[:, k, :], abre[:], ALU.mult)
            tt_(t1[:], pim[:, k, :], abim[:], ALU.mult)
            tt_(pre[:, k + 1, :], t0[:], t1[:], ALU.subtract)
            tt_(t0[:], pre[:, k, :], abim[:], ALU.mult)
            tt_(t1[:], pim[:, k, :], abre[:], ALU.mult)
            tt_(pim[:, k + 1, :], t0[:], t1[:], ALU.add)
        return pre, pim

    l2 = sb(st, "l2", [128, 3, 16], F32)
    c2t = sb(st, "c2t_", [128, 2, 16, 32], F32)
    b2t = sb(st, "b2t", [128, 2, 16, 128], F32)
    kb.dma("sp", l2[:], lam2, writes=[bs])
    kb.dma("sp", c2t[:], c2, writes=[bs])
    kb.dma("sp", b2t[:], b2, writes=[bs])
    abre, abim, zre, zim, tm = lam_calc(l2[:, 0, :], l2[:, 1, :], l2[:, 2, :], 16, "a2")
    pre, pim = powers(abre, abim, 16, T + 1, "a2", tm)
    kb.cp(E, ld[:, :, 0, 0], pre[:, T, :], [bs], [bs, b_ld])
    kb.cp(E, ld[:, :, 0, 1], pim[:, T, :], [bs], [bs, b_ld])
    for k in range(NLD):
        tt_(tm[0][:], ld[:, :, k, 0], ld[:, :, k, 0], ALU.mult)
        tt_(tm[1][:], ld[:, :, k, 1], ld[:, :, k, 1], ALU.mult)
        tt_(ld[:, :, k + 1, 0], tm[0][:], tm[1][:], ALU.subtract)
        stt_(ld[:, :, k + 1, 1], ld[:, :, k, 0], 2.0, ld[:, :, k, 1], ALU.mult, ALU.mult)
    kb.ts(E, ld[:, :, :, 2], ld[:, :, :, 1], -1.0, None, ALU.mult, None, [bs], [bs, b_ld])
    bst = sb(st, "bst", [128, 16, 2, 128], BF16)
    w1 = sb(st, "w1", [128, 16, 128], F32)
    w2 = sb(st, "w2", [128, 16, 128], F32)
    zre_b = zre[:, :].unsqueeze(2).to_broadcast([128, 16, 128])
    zim_b = zim[:, :].unsqueeze(2).to_broadcast([128, 16, 128])
    tt_(w1[:], b2t[:, 0, :, :], zre_b, ALU.mult)
    tt_(w2[:], b2t[:, 1, :, :], zim_b, ALU.mult)
    tt_(bst[:, :, 0, :], w1[:], w2[:], ALU.subtract)
    tt_(w1[:], b2t[:, 0, :, :], zim_b, ALU.mult)
    tt_(w2[:], b2t[:, 1, :, :], zre_b, ALU.mult)
    tt_(bst[:, :, 1, :], w1[:], w2[:], ALU.add)
    cps = sb(st, "cps", [128, 16, T + 1, 2, 32], BF16)
    v1 = sb(st, "v1", [128, 16, 32], F32)
    v2 = sb(st, "v2", [128, 16, 32], F32)
    for k in range(T + 1):
        pr_b = pre[:, k, :].unsqueeze(2).to_broadcast([128, 16, 32])
        pi_b = pim[:, k, :].unsqueeze(2).to_broadcast([128, 16, 32])
        tt_(v1[:], c2t[:, 0, :, :], pr_b, ALU.mult)
        tt_(v2[:], c2t[:, 1, :, :], pi_b, ALU.mult)
        tt_(cps[:, :, k, 0, :], v1[:], v2[:], ALU.subtract)
        tt_(v1[:], c2t[:, 0, :, :], pi_b, ALU.mult)
        tt_(v2[:], c2t[:, 1, :, :], pr_b, ALU.mult)
        stt_(cps[:, :, k, 1, :], v1[:], -1.0, v2[:], ALU.mult, ALU.subtract)
    kb.dma("sp", cpt, cps[:], reads=[bs], writes=[b_cpt])
    l1 = sb(st, "l1", [128, 3, 256], F32)
    b1t = sb(st, "b1t", [128, 2, 4, 128], F32)
    kb.dma("sp", l1[:], lam1.rearrange("p a f q -> p a (f q)"), writes=[bs])
    kb.dma("sp", b1t[:], b1, writes=[bs])
    abre1, abim1, zre1, zim1, tm1 = lam_calc(l1[:, 0, :], l1[:, 1, :], l1[:, 2, :], 256, "a1")
    cur_re = sb(st, "cur_re", [128, 256], F32)
    cur_im = sb(st, "cur_im", [128, 256], F32)
    nxt_re = sb(st, "nxt_re", [128, 256], F32)
    kb.op(E, lambda e: e.memset(cur_re[:], 1.0), [bs], [bs])
    kb.op(E, lambda e: e.memset(cur_im[:], 0.0), [bs], [bs])
    bbre = sb(st, "bbre", [128, 4, 2, 64], F32)
    bbim = sb(st, "bbim", [128, 4, 2, 64], F32)
    x1_ = sb(st, "x1_", [128, 4, 2, 64], F32)
    x2_ = sb(st, "x2_", [128, 4, 2, 64], F32)

    def bq(t):
        return t.rearrange("p (f q) -> p f q", f=4).unsqueeze(2).to_broadcast([128, 4, 2, 64])

    def v4(t):
        return t.rearrange("p f (t q) -> p f t q", t=2)

    tt_(x1_[:], v4(b1t[:, 0, :, :]), bq(zre1[:, :]), ALU.mult)
    tt_(x2_[:], v4(b1t[:, 1, :, :]), bq(zim1[:, :]), ALU.mult)
    tt_(bbre[:], x1_[:], x2_[:], ALU.subtract)
    tt_(x1_[:], v4(b1t[:, 0, :, :]), bq(zim1[:, :]), ALU.mult)
    tt_(x2_[:], v4(b1t[:, 1, :, :]), bq(zre1[:, :]), ALU.mult)
    tt_(bbim[:], x1_[:], x2_[:], ALU.add)
    bps = sb(st, "bps", [128, 4, T, 2, 128], BF16)
    for i in range(T):
        if i > 0:
            t0_, t1_ = tm1[0], tm1[1]
            tt_(t0_[:], cur_re[:], abre1[:], ALU.mult)
            tt_(t1_[:], cur_im[:], abim1[:], ALU.mult)
            tt_(nxt_re[:], t0_[:], t1_[:], ALU.subtract)
            tt_(t0_[:], cur_re[:], abim1[:], ALU.mult)
            tt_(t1_[:], cur_im[:], abre1[:], ALU.mult)
            tt_(cur_im[:], t0_[:], t1_[:], ALU.add)
            kb.cp(E, cur_re[:], nxt_re[:], [bs], [bs])
        pr_b = bq(cur_re[:, :])
        pi_b = bq(cur_im[:, :])
        tt_(x1_[:], bbre[:], pr_b, ALU.mult)
        tt_(x2_[:], bbim[:], pi_b, ALU.mult)
        tt_(v4(bps[:, :, i, 0, :]), x1_[:], x2_[:], ALU.subtract)
        tt_(x1_[:], bbim[:], pr_b, ALU.mult)
        tt_(x2_[:], bbre[:], pi_b, ALU.mult)
        tt_(v4(bps[:, :, i, 1, :]), x1_[:], x2_[:], ALU.add)
    kb.dma("sp", bp, bps[:], reads=[bs], writes=[b_bp])

    if mid is not None:
        mid()
    psK = pm(st, "psK", [128, T, 128], F32); b_psK = Buf()
    kls = sb(st, "kls", [128, T, 128], BF16); b_kls = Buf()
    for ft in range(4):
        for tau in range(T):
            for pr in range(4):
                pair = 4 * ft + pr
                kb.mm(psK[:, tau, 32 * pr:32 * pr + 32], bst[:, pair, 0, :], cps[:, pair, tau, 0, :], True, False,
                      reads=[bs], writes=[b_psK], signal=False)
                kb.mm(psK[:, tau, 32 * pr:32 * pr + 32], bst[:, pair, 1, :], cps[:, pair, tau, 1, :], False, True,
                      reads=[bs], writes=[b_psK], signal=(pr == 3 and tau == T - 1))
        kb.cp("dve", kls[:, 1:T, :], psK[:, 1:T, :], [b_psK], [b_kls])
        a_, _ = PP["ssmd"]
        kb.stt("dve", kls[:, 0, :], ident_f[:], ppt[:, a_ + ft:a_ + ft + 1], psK[:, 0, :], ALU.mult, ALU.add,
               [b_psK, b_c, b_pp], [b_kls, b_psK])
        kb.dma("sp", klag[:, ft], kls[:], reads=[b_kls], writes=[b_klag])


def phase3(kb, nc, st, sb, pm, ring, E):
    g = E
    S, NT, NC, NLD = g["S"], g["NT"], g["NC"], g["NLD"]
    ppt, P_ = g["ppt"], g["P_"]
    b_pp, b_c = g["b_pp"], g["b_c"]
    ones_b = g["ones_b"]
    klag, bp, cpt, ld = g["klag"], g["bp"], g["cpt"], g["ld"]
    b_klag, b_bp, b_cpt, b_ld = g["b_klag"], g["b_bp"], g["b_cpt"], g["b_ld"]
    dU, bU, dSN, bSN = g["dU"], g["bU"], g["dSN"], g["bSN"]

    WGLU = sb(st, "WGLU", [128, 4, 1024], BF16); b_w = Buf()
    kb.dma("sp", WGLU[:], g["dWGLU"].rearrange("(kt p) n -> p kt n", p=128), reads=[g["bWc"]], writes=[b_w])
    Gn = sb(st, "Gn", [128, 4, S], BF16); b_gn = [Buf() for _ in range(4)]
    psV = ring(st, "psV", [128, 512], F32, 2, psum=True)
    psY = ring(st, "psY", [128, 512], F32, 2, psum=True)
    pz = ring(st, "pz3", [128, 512], F32, 3, psum=True)
    psN = ring(st, "psN3", [128, 512], F32, 1, psum=True)

    with ExitStack() as st2:
        KL = ring(st2, "KL", [128, T, 128], BF16, 2)
        BPs = ring(st2, "BPs", [128, T, 2, 128], BF16, 2)
        CPs = ring(st2, "CPs", [128, 4, T + 1, 2, 32], BF16, 2)
        Unat = ring(st2, "Unat", [128, S], BF16, 1)
        Ujm = ring(st2, "Ujm", [128, T, NC], BF16, 2)
        S16 = ring(st2, "S16", [128, 4, 2, NC + 1], BF16, 2)
        scA = [ring(st2, "scA%d" % i, [128, 2, NC], F32, 1) for i in range(2)]
        scB = [ring(st2, "scB%d" % i, [128, 2, NC], F32, 1) for i in range(2)]

        def load(ft):
            c = {"ft": ft}
            c["kl"] = KL.next(); c["bp"] = BPs.next(); c["cp"] = CPs.next()
            un_t, un_b = Unat.next()
            c["uj"] = Ujm.next(); c["s16"] = S16.next()
            kb.dma("sp", c["kl"][0][:], klag[:, ft], reads=[b_klag], writes=[c["kl"][1]])
            kb.dma("sp", c["bp"][0][:], bp[:, ft], reads=[b_bp], writes=[c["bp"][1]])
            kb.dma("sp", c["cp"][0][:], cpt[:, 4 * ft:4 * ft + 4], reads=[b_cpt], writes=[c["cp"][1]])
            kb.dma("sp", un_t[:], dU[:, ft, :], reads=bU, writes=[un_b])
            kb.cp("dve", c["uj"][0][:], un_t[:, :].rearrange("p (c j) -> p j c", j=T), [un_b], [c["uj"][1]])
            kb.op("dve", lambda e, t=c["s16"][0]: e.memset(t[:, :, :, 0:1], 0.0), writes=[c["s16"][1]])
            return c

        def scan_gen(c):
            ft = c["ft"]
            bp_t, bp_b = c["bp"]; uj_t, uj_b = c["uj"]; s16_t, s16_b = c["s16"]
            for pr0 in (0, 2):
                bufs = []
                for pr in (pr0, pr0 + 1):
                    a_t, a_b = scA[pr % 2].next()
                    b_t, b_b = scB[pr % 2].next()
                    for ri in range(2):
                        v_t, v_b = psV.next()
                        for i in range(T):
                            kb.mm(v_t[:, 0:NC], bp_t[32 * pr:32 * pr + 32, i, ri, :],
                                  uj_t[32 * pr:32 * pr + 32, T - 1 - i, :], i == 0, i == T - 1,
                                  reads=[bp_b, uj_b], writes=[v_b], signal=(i == T - 1), tp=(32 * pr, 0))
                        kb.cp("act", a_t[:, ri, :], v_t[:, 0:NC], [v_b], [a_b])
                    bufs.append([pr, a_t, a_b, b_t, b_b])
                yield
                for k in range(NLD):
                    d = 1 << k
                    n = NC
                    for step in range(5):
                        for bf in bufs:
                            pr, src, src_b, dst, dst_b = bf
                            pair = 4 * ft + pr
                            lre = ld[:, pair, k, 0:1]
                            lim = ld[:, pair, k, 1:2]
                            nlim = ld[:, pair, k, 2:3]
                            if step == 0:
                                kb.stt("dve", dst[:, 0, d:n], src[:, 1, 0:n - d], nlim, src[:, 0, d:n], ALU.mult, ALU.add,
                                       [src_b, b_ld], [dst_b])
                            elif step == 1:
                                kb.stt("dve", dst[:, 0, d:n], src[:, 0, 0:n - d], lre, dst[:, 0, d:n], ALU.mult, ALU.add,
                                       [src_b, b_ld, dst_b], [dst_b])
                            elif step == 2:
                                kb.stt("dve", dst[:, 1, d:n], src[:, 0, 0:n - d], lim, src[:, 1, d:n], ALU.mult, ALU.add,
                                       [src_b, b_ld, dst_b], [dst_b])
                            elif step == 3:
                                kb.stt("dve", dst[:, 1, d:n], src[:, 1, 0:n - d], lre, dst[:, 1, d:n], ALU.mult, ALU.add,
                                       [src_b, b_ld, dst_b], [dst_b])
                            else:
                                kb.cp("act", dst[:, :, 0:d], src[:, :, 0:d], [src_b, dst_b], [dst_b])
                    for bf in bufs:
                        bf[1], bf[2], bf[3], bf[4] = bf[3], bf[4], bf[1], bf[2]
                    yield
                for bf in bufs:
                    pr, src, src_b, dst, dst_b = bf
                    kb.cp("act", s16_t[:, pr, :, 1:NC + 1], src[:, :, :], [src_b], [s16_b])
                yield

        def ad_gen(c):
            ft = c["ft"]
            kl_t, kl_b = c["kl"]; cp_t, cp_b = c["cp"]; uj_t, uj_b = c["uj"]; s16_t, s16_b = c["s16"]
            gview = Gn[:, ft, :].rearrange("p (c j) -> p j c", j=T)
            for j in range(T):
                y_t, y_b = psY.next()
                for tau in range(j + 1):
                    kb.mm(y_t[:, 0:NC], kl_t[:, tau, :], uj_t[:, j - tau, :], tau == 0, False,
                          reads=[kl_b, uj_b], writes=[y_b], signal=False)
                for pr in range(4):
                    for ri in range(2):
                        last = (pr == 3 and ri == 1)
                        kb.mm(y_t[32 * pr:32 * pr + 32, 0:NC], cp_t[:, pr, j + 1, ri, :], s16_t[:, pr, ri, 0:NC], False, last,
                              reads=[cp_b, s16_b], writes=[y_b], signal=last, tp=(0, 32 * pr))
                kb.act(gview[:, j, :], y_t[:, 0:NC], AF.Gelu_apprx_tanh, [y_b], [b_gn[ft]])
                yield

        def interleave(*gs):
            gens = [x for x in gs if x is not None]
            while gens:
                for x in list(gens):
                    try:
                        next(x)
                    except StopIteration:
                        gens.remove(x)

        prev = None
        for ft in range(4):
            c = load(ft)
            interleave(scan_gen(c), ad_gen(prev) if prev is not None else None)
            prev = c
        interleave(ad_gen(prev))

    sgm = ring(st, "sgm", [128, 512], F32, 2)
    sf = ring(st, "sf", [128, 4, 512], F32, 1)
    sq3 = ring(st, "sq3", [128, 512], BF16, 2)
    rs3 = ring(st, "rs3", [128, 512], F32, 1)
    sn = ring(st, "sn", [128, 4, 512], BF16, 2)
    for tt in range(NT):
        t0 = tt * 512
        f_t, f_b = sf.next()
        for m in range(4):
            pa, pa_b = pz.next()
            pb, pb_b = pz.next()
            for k in range(4):
                kb.mm(pa[:], WGLU[:, k, m * 128:(m + 1) * 128], Gn[:, k, t0:t0 + 512], k == 0, k == 3,
                      reads=[b_w] + b_gn, writes=[pa_b], signal=(k == 3))
            for k in range(4):
                kb.mm(pb[:], WGLU[:, k, 512 + m * 128:512 + (m + 1) * 128], Gn[:, k, t0:t0 + 512], k == 0, k == 3,
                      reads=[b_w] + b_gn, writes=[pb_b], signal=(k == 3))
            g_t, g_b = sgm.next()
            kb.act(g_t[:], pb[:], AF.Sigmoid, [pb_b, b_pp], [g_b], bias=P_("bglu", 4 + m))
            kb.stt("dve", f_t[:, m, :], pa[:], P_("bglu", m), g_t[:], ALU.add, ALU.mult, [pa_b, g_b, b_pp], [f_b])
        n_t, n_b = psN.next()
        for m in range(4):
            q_t, q_b = sq3.next()
            kb.tt("dve", q_t[:], f_t[:, m, :], f_t[:, m, :], ALU.mult, [f_b], [q_b])
            kb.mm(n_t[:], ones_b[:], q_t[:], m == 0, m == 3, reads=[q_b, b_c], writes=[n_b], signal=(m == 3))
        r_t, r_b = rs3.next()
        kb.act(r_t[:], n_t[:], AF.Sqrt, [n_b, b_pp], [r_b], scale=1.0 / 512, bias=P_("eps"))
        kb.op("dve", lambda e, r_t=r_t: e.reciprocal(out=r_t[:], in_=r_t[:]), [r_b], [r_b])
        s_t, s_b = sn.next()
        for m in range(4):
            kb.stt("dve", s_t[:, m, :], f_t[:, m, :], P_("sg", m), r_t[:], ALU.mult, ALU.mult, [f_b, r_b, b_pp], [s_b])
        kb.dma("sp", dSN[:, :, t0:t0 + 512], s_t[:], reads=[s_b], writes=[bSN[tt]])


def _pm(v, n):
    return np.ascontiguousarray(np.asarray(v, np.float32).reshape(n, 128).T)


def prep_shared(inp):
    f = lambda k: np.asarray(inp[k], np.float32)
    m = {}
    pp = np.zeros((128, NPP), np.float32)

    def put(name, arr):
        a, b = PP[name]
        pp[:, a:b] = arr.reshape(128, b - a)

    put("bmod", _pm(f("b_mod")[0], 48))
    put("ln1", _pm(f("ln1_g")[0], 8))
    put("ln2", _pm(f("ln2_g")[0], 8))
    put("qg", _pm(f("q_norm_g")[0], 3))
    put("kvg", _pm(f("kv_norm_g")[0], 2))
    put("ag", _pm(f("attn_out_g")[0], 4))
    put("sg", _pm(f("ssm_out_g")[0], 4))
    put("bglu", _pm(f("b_glu")[0], 8))
    put("convw", np.ascontiguousarray(f("conv_w")[0].reshape(3, NKF, 128).transpose(2, 1, 0)).reshape(128, 66))
    put("convb", _pm(f("conv_b")[0], NKF))
    inv_freq = (np.float32(10000.0) ** (-np.arange(0, 64, 2, dtype=np.float32) / np.float32(64))).astype(np.float32)
    p = np.arange(128)
    put("invf", inv_freq[p % 32].reshape(128, 1))
    put("sgn", np.where((p % 64) < 32, -1.0, 1.0).astype(np.float32).reshape(128, 1))
    put("ssmd", np.ascontiguousarray(f("ssm_d")[0].reshape(4, 128).T))
    put("halfpi", np.full((128, 1), math.pi / 2, np.float32))
    put("eps", np.full((128, 1), EPS, np.float32))
    m["pp"] = pp
    m["bmod_row"] = f("b_mod")[0].reshape(1, 6 * D)
    m["fing"] = f("final_g").reshape(1, D)
    m["w_mod"] = f("w_mod")[0]
    w_in = f("w_in")[0]
    sw = np.concatenate([np.arange(672, 704), np.arange(640, 672)])
    m["w_in"] = np.ascontiguousarray(np.concatenate([w_in[:, :704], w_in[:, sw], w_in[:, 704:]], 1))
    w_uq = f("w_uq")[0]
    cols = []
    for h in range(4):
        cols.append(np.arange(h * 192, h * 192 + 128))
    for h in range(4):
        cols.append(np.arange(h * 192 + 128, h * 192 + 192))
    for h in range(4):
        cols.append(np.concatenate([np.arange(h * 192 + 160, h * 192 + 192), np.arange(h * 192 + 128, h * 192 + 160)]))
    m["w_uq"] = np.ascontiguousarray(w_uq[:, np.concatenate(cols)])
    w_ukv = f("w_ukv")[0]
    m["w_ukn"] = np.ascontiguousarray(np.concatenate([w_ukv[:, h * 256:h * 256 + 128] for h in range(4)], 1))
    m["w_uv"] = np.ascontiguousarray(np.concatenate([w_ukv[:, h * 256 + 128:h * 256 + 256] for h in range(4)], 1))
    m["w_glu"] = f("w_glu")[0]
    m["w_out"] = f("w_out")[0]
    w_up = f("w_up")[0]
    il = []
    for kt in range(NKF):
        il.append(np.arange(kt * 128, kt * 128 + 128))
        il.append(np.arange(DFF + kt * 128, DFF + kt * 128 + 128))
    m["w_up"] = np.ascontiguousarray(w_up[:, np.concatenate(il)])
    m["w_down"] = f("w_down")[0]
    lre, lim, ldt = f("ssm_lam_re")[0], f("ssm_lam_im")[0], f("ssm_log_dt")[0]
    bre, bim = f("ssm_b_re")[0], f("ssm_b_im")[0]
    cre, cim = f("ssm_c_re")[0], f("ssm_c_im")[0]
    lam2 = np.zeros((128, 3, 16), np.float32)
    c2 = np.zeros((128, 2, 16, 32), np.float32)
    b2 = np.zeros((128, 2, 16, 128), np.float32)
    for pair in range(16):
        for two in range(2):
            g_ = 2 * pair + two
            rows = slice(two * 64, two * 64 + 64)
            lam2[rows, 0, pair] = lre[g_]
            lam2[rows, 1, pair] = lim[g_]
            lam2[rows, 2, pair] = ldt[g_]
            c2[rows, 0, pair, two * 16:(two + 1) * 16] = cre[g_].T
            c2[rows, 1, pair, two * 16:(two + 1) * 16] = cim[g_].T
            c0 = 32 * (pair % 4) + 16 * two
            b2[rows, 0, pair, c0:c0 + 16] = bre[g_]
            b2[rows, 1, pair, c0:c0 + 16] = bim[g_]
    lam1 = np.zeros((128, 3, 4, 64), np.float32)
    b1 = np.zeros((128, 2, 4, 128), np.float32)
    for ft in range(4):
        for g8 in range(8):
            g_ = ft * 8 + g8
            rows = slice(g8 * 16, g8 * 16 + 16)
            two = g8 % 2
            lam1[rows, 0, ft, :] = lre[g_][None, :]
            lam1[rows, 1, ft, :] = lim[g_][None, :]
            lam1[rows, 2, ft, :] = ldt[g_]
            b1[rows, 0, ft, two * 64:(two + 1) * 64] = bre[g_].T
            b1[rows, 1, ft, two * 64:(two + 1) * 64] = bim[g_].T
    m["lam2"], m["c2"], m["b2"], m["lam1"], m["b1"] = lam2, c2, b2, lam1, b1
    cst = np.zeros((128, 4, 128), np.float32)
    cst[:, 0, :] = np.eye(128, dtype=np.float32)
    pi_, ci_ = np.meshgrid(np.arange(128), np.arange(128), indexing="ij")
    cst[:, 1, :] = np.where(pi_ <= ci_, 0.0, -30000.0)
    cst[0:64, 2, :] = 1.0
    cst[64:128, 3, :] = 1.0
    m["cst"] = cst
    return m


def prep_core(inp, b, S, shared):
    m = dict(shared)
    m["x"] = np.ascontiguousarray(np.asarray(inp["x"], np.float32)[b, :S])
    m["pos"] = np.ascontiguousarray(np.asarray(inp["positions"]).astype(np.int32)[b:b + 1, :S])
    m["cpm"] = _pm(np.asarray(inp["c"], np.float32)[b], 8)
    return m


_NC_CACHE = {}


def kernel(**inputs):
    S = inputs["x"].shape[1]
    B = inputs["x"].shape[0]
    if S not in _NC_CACHE:
        _NC_CACHE[S] = build(S)[0]
    nc = _NC_CACHE[S]
    shared = prep_shared(inputs)
    in_maps = [prep_core(inputs, b, S, shared) for b in range(B)]
    res = run_bass_kernel_spmd(nc, in_maps, core_ids=list(range(B)))
    return np.stack([np.asarray(r["out"], np.float32) for r in res.results], 0)
```
